# Optimizing a Trainium2 kernel written in Bass

```python
import jax, jax.numpy as jnp
from jax import lax
import numpy as np

D_MODEL = 1024
BATCH = 8
SEQ = 8192
DEPTH = 1

MIX_WIDTH = D_MODEL
POOL_WIDTH = MIX_WIDTH // 2
POOL_WINDOWS = (2, 4, 8, 16)
N_POOL_GROUPS = len(POOL_WINDOWS)
POOL_GROUP = POOL_WIDTH // N_POOL_GROUPS
HGRN_WIDTH = MIX_WIDTH - POOL_WIDTH
HGRN_HEAD_DIM = 128
HGRN_HEADS = HGRN_WIDTH // HGRN_HEAD_DIM
CHUNK = 64
D_FF = 2816
EPS = 1e-6
IN_COLS = POOL_WIDTH + 5 * HGRN_WIDTH

kernel_name = "hybrid_pool_hgrn2_macaron_encoder"


def rmsnorm(x, w):
    xf = x.astype(jnp.float32)
    y = xf * lax.rsqrt(jnp.mean(xf * xf, axis=-1, keepdims=True) + EPS)
    return (y * w.astype(jnp.float32)).astype(x.dtype)


def swiglu(x, w_gate, w_up, w_down):
    return (jax.nn.silu(x @ w_gate) * (x @ w_up)) @ w_down


def centred_mean_minus_self(z, window):
    S = z.shape[1]
    zf = z.astype(jnp.float32)
    csum = jnp.concatenate([jnp.zeros_like(zf[:, :1]), jnp.cumsum(zf, axis=1)], axis=1)
    t = jnp.arange(S)
    lo = jnp.clip(t - window // 2, 0, S)
    hi = jnp.clip(t + window - window // 2, 0, S)
    total = jnp.take(csum, hi, axis=1) - jnp.take(csum, lo, axis=1)
    count = (hi - lo).astype(jnp.float32)[None, :, None]
    return (total / count - zf).astype(z.dtype)


def pool_mixer(z, pool_w, pool_scale):
    B, S, _ = z.shape
    zg = z.reshape(B, S, N_POOL_GROUPS, POOL_GROUP)
    pooled = jnp.stack([centred_mean_minus_self(zg[:, :, g], w) for g, w in enumerate(POOL_WINDOWS)], axis=2)
    y = jnp.einsum('bsgc,gcd->bsgd', pooled, pool_w)
    return y.reshape(B, S, POOL_WIDTH) * pool_scale


def gated_linear_recurrence(q, k, v, log_f):
    B, N, S, DK = q.shape
    DV = v.shape[-1]
    nc = S // CHUNK

    def to_chunks(a):
        return a.reshape(B, N, nc, CHUNK, a.shape[-1]).transpose(2, 0, 1, 3, 4)

    qc, kc, vc, gc = to_chunks(q), to_chunks(k), to_chunks(v), to_chunks(log_f)
    lower_tri = jnp.tril(jnp.ones((CHUNK, CHUNK), dtype=bool))

    def step(state, inp):
        q_, k_, v_, g_ = inp
        b = jnp.cumsum(g_, axis=2)
        o_inter = jnp.einsum('bnck,bnkv->bncv', q_ * jnp.exp(b), state)
        diff = b[:, :, :, None, :] - b[:, :, None, :, :]
        decay = jnp.exp(jnp.where(lower_tri[:, :, None], diff, -jnp.inf))
        scores = jnp.einsum('bnik,bnijk,bnjk->bnij', q_, decay, k_)
        o = o_inter + jnp.einsum('bnij,bnjv->bniv', scores, v_)
        b_last = b[:, :, -1:, :]
        k_dec = k_ * jnp.exp(b_last - b)
        state = state * jnp.exp(b_last[:, :, 0, :, None]) + jnp.einsum('bnck,bncv->bnkv', k_dec, v_)
        return state, o

    s0 = jnp.zeros((B, N, DK, DV), jnp.float32)
    _, o = lax.scan(step, s0, (qc, kc, vc, gc))
    return o.transpose(1, 2, 0, 3, 4).reshape(B, N, S, DV)


def hgrn2_bidirectional(q_raw, i_raw, f_fwd_raw, f_bwd_raw, g_raw, hgrn_lb, gnorm_w, layer_idx):
    B, S, _ = q_raw.shape
    N = HGRN_HEADS
    lb = jnp.cumsum(jax.nn.softmax(hgrn_lb.astype(jnp.float32), axis=1), axis=1)[:, layer_idx]

    def heads(a):
        return a.reshape(B, S, N, HGRN_HEAD_DIM).transpose(0, 2, 1, 3)

    def gate(f_raw, lb_dir):
        f = lb_dir + (1.0 - lb_dir) * jax.nn.sigmoid(f_raw.astype(jnp.float32))
        return heads(jnp.log(f)), heads(1.0 - f)

    q = heads(jax.nn.silu(q_raw.astype(jnp.float32)))
    v = heads(i_raw.astype(jnp.float32))
    logf_fwd, k_fwd = gate(f_fwd_raw, lb[0])
    logf_bwd, k_bwd = gate(f_bwd_raw, lb[1])

    def flip(a):
        return jnp.flip(a, axis=2)

    qq = jnp.concatenate([q, flip(q)], axis=1)
    kk = jnp.concatenate([k_fwd, flip(k_bwd)], axis=1)
    vv = jnp.concatenate([v, flip(v)], axis=1)
    gg = jnp.concatenate([logf_fwd, flip(logf_bwd)], axis=1)
    o = gated_linear_recurrence(qq, kk, vv, gg)
    o = o[:, :N] + flip(o[:, N:])
    o = o.transpose(0, 2, 1, 3)
    o = o * lax.rsqrt(jnp.mean(o * o, axis=-1, keepdims=True) + EPS) * gnorm_w.astype(jnp.float32)
    o = o.reshape(B, S, HGRN_WIDTH) * jax.nn.silu(g_raw.astype(jnp.float32))
    return o.astype(q_raw.dtype)


def setup_inputs(seed: int = 0) -> dict:
    key = jax.random.key(seed)
    ks = jax.random.split(key, 20)
    f32 = jnp.float32

    def nrm(k, shape, fan_in):
        return jax.random.normal(k, shape, f32) * (fan_in ** -0.5)

    def gain(k, shape):
        return 1.0 + 0.02 * jax.random.normal(k, shape, f32)

    return {
        "x": jax.random.normal(ks[0], (BATCH, SEQ, D_MODEL), f32),
        "norm_ffn1": gain(ks[1], (DEPTH, D_MODEL)),
        "w_ffn1_gate": nrm(ks[2], (DEPTH, D_MODEL, D_FF), D_MODEL),
        "w_ffn1_up": nrm(ks[3], (DEPTH, D_MODEL, D_FF), D_MODEL),
        "w_ffn1_down": nrm(ks[4], (DEPTH, D_FF, D_MODEL), D_FF),
        "norm_mix": gain(ks[5], (DEPTH, D_MODEL)),
        "w_in": nrm(ks[6], (DEPTH, D_MODEL, IN_COLS), D_MODEL),
        "pool_w": nrm(ks[7], (DEPTH, N_POOL_GROUPS, POOL_GROUP, POOL_GROUP), POOL_GROUP),
        "pool_scale": gain(ks[8], (DEPTH, POOL_WIDTH)),
        "hgrn_lb": 0.1 * jax.random.normal(ks[9], (2, DEPTH + 1, HGRN_WIDTH), f32),
        "hgrn_gnorm": gain(ks[10], (DEPTH, HGRN_HEAD_DIM)),
        "w_out": nrm(ks[11], (DEPTH, MIX_WIDTH, D_MODEL), MIX_WIDTH),
        "norm_ffn2": gain(ks[12], (DEPTH, D_MODEL)),
        "w_ffn2_gate": nrm(ks[13], (DEPTH, D_MODEL, D_FF), D_MODEL),
        "w_ffn2_up": nrm(ks[14], (DEPTH, D_MODEL, D_FF), D_MODEL),
        "w_ffn2_down": nrm(ks[15], (DEPTH, D_FF, D_MODEL), D_FF),
        "norm_final": gain(ks[16], (D_MODEL,)),
    }


def reference(x, norm_ffn1, w_ffn1_gate, w_ffn1_up, w_ffn1_down, norm_mix, w_in, pool_w, pool_scale,
              hgrn_lb, hgrn_gnorm, w_out, norm_ffn2, w_ffn2_gate, w_ffn2_up, w_ffn2_down, norm_final):
    split_at = [POOL_WIDTH + j * HGRN_WIDTH for j in range(5)]
    h = x
    for l in range(DEPTH):
        h = h + 0.5 * swiglu(rmsnorm(h, norm_ffn1[l]), w_ffn1_gate[l], w_ffn1_up[l], w_ffn1_down[l])
        u = rmsnorm(h, norm_mix[l])
        proj = u @ w_in[l]
        z_pool, q_raw, i_raw, f_fwd_raw, f_bwd_raw, g_raw = jnp.split(proj, split_at, axis=-1)
        y_pool = pool_mixer(z_pool, pool_w[l], pool_scale[l])
        y_hgrn = hgrn2_bidirectional(q_raw, i_raw, f_fwd_raw, f_bwd_raw, g_raw,
                                     hgrn_lb, hgrn_gnorm[l], l)
        h = h + jnp.concatenate([y_pool, y_hgrn], axis=-1) @ w_out[l]
        h = h + 0.5 * swiglu(rmsnorm(h, norm_ffn2[l]), w_ffn2_gate[l], w_ffn2_up[l], w_ffn2_down[l])
    return rmsnorm(h, norm_final)
```

```python
import numpy as np
from contextlib import ExitStack
import concourse.bass as bass
import concourse.mybir as mybir
from concourse.bass_utils import run_bass_kernel_spmd

F32 = mybir.dt.float32
BF16 = mybir.dt.bfloat16
AF = mybir.ActivationFunctionType
ALU = mybir.AluOpType

D = 1024
DFF = 2816
NF = DFF // 128
NK = D // 128
SEQ = 8192
NCORES = 8
EPS = 1e-6
POOL_WINDOWS = (2, 4, 8, 16)


class _Op:
    __slots__ = ("eng", "fn", "deps", "dma", "needs_inc", "semkey", "val", "waits")


class Prog:
    COMPUTE = ("pe", "act", "dve", "pool")

    def __init__(self):
        self.ops = []
        self.lw = {}
        self.rd = {}

    def add(self, eng, fn, r=(), w=(), dma=None):
        op = _Op()
        op.eng, op.fn, op.dma = eng, fn, dma
        op.needs_inc = False
        op.semkey = None
        op.val = 0
        deps = []
        for k in r:
            x = self.lw.get(k)
            if x is not None:
                deps.append(x)
        for k in w:
            x = self.lw.get(k)
            if x is not None:
                deps.append(x)
            rr = self.rd.get(k)
            if rr:
                for v in rr.values():
                    if isinstance(v, list):
                        deps.extend(v)
                    else:
                        deps.append(v)
        for k in r:
            rr = self.rd.setdefault(k, {})
            if dma is not None:
                rr.setdefault("_dma", []).append(op)
            else:
                rr[eng] = op
        for k in w:
            self.lw[k] = op
            self.rd[k] = {}
        dd = []
        seen = set()
        for d_ in deps:
            if d_ is op or id(d_) in seen:
                continue
            seen.add(id(d_))
            if d_.eng == "pe" and eng == "pe" and d_.dma is None and dma is None:
                continue
            d_.needs_inc = True
            dd.append(d_)
        op.deps = dd
        self.ops.append(op)
        return op

    def finalize(self):
        cnt = {}
        waited = {}
        for op in self.ops:
            key = ("dma", op.dma) if op.dma is not None else ("eng", op.eng)
            need = {}
            for d_ in op.deps:
                if d_.val > need.get(d_.semkey, 0):
                    need[d_.semkey] = d_.val
            ws = []
            wd = waited.setdefault(op.eng, {})
            for sk, v in need.items():
                if wd.get(sk, 0) >= v:
                    continue
                wd[sk] = v
                ws.append((sk, v))
            op.waits = ws
            if op.needs_inc:
                cnt[key] = cnt.get(key, 0) + (16 if op.dma is not None else 1)
                op.semkey = key
                op.val = cnt[key]
        return list(cnt.keys())

    def emit(self, nc, es):
        keys = self.finalize()
        sems = {}
        for i, k in enumerate(keys):
            sems[k] = es.enter_context(nc.semaphore("s%d" % i))
        block = es.enter_context(nc.Block())
        per = {}
        for op in self.ops:
            per.setdefault(op.eng, []).append(op)

        def run(eng_name):
            def body(e):
                for op in per.get(eng_name, []):
                    for sk, v in op.waits:
                        e.wait_ge(sems[sk], v)
                    if op.fn is None:
                        continue
                    ins = op.fn(e)
                    if op.needs_inc:
                        ins.then_inc(sems[op.semkey], 16 if op.dma is not None else 1)
            return body

        block.sync(run("sp"))
        block.scalar(run("act"))
        block.vector(run("dve"))
        block.gpsimd(run("pool"))
        block.tensor(run("pe"))


def _pool_tables():
    Sv = 384
    B = np.zeros((3, 3, 4, 128, 128), np.float32)
    rc = np.zeros((3, 4, 128), np.float32)
    for g, w in enumerate(POOL_WINDOWS):
        full = np.zeros((Sv, Sv), np.float32)
        cnts = np.zeros(Sv, np.float32)
        for t in range(Sv):
            lo = min(max(t - w // 2, 0), Sv)
            hi = min(max(t + w - w // 2, 0), Sv)
            full[lo:hi, t] = 1.0
            cnts[t] = hi - lo
            full[t, t] -= cnts[t]
        for var in range(3):
            c = var
            rc[var, g] = 1.0 / cnts[c * 128:(c + 1) * 128]
            for pos in range(3):
                sc = c + pos - 1
                if 0 <= sc < 3:
                    B[var, pos, g] = full[sc * 128:(sc + 1) * 128, c * 128:(c + 1) * 128]
    return B, rc


def _consts():
    ident = np.eye(128, dtype=np.float32)
    j = np.arange(128)[:, None]
    t = np.arange(128)[None, :]
    maskf = (j <= t).astype(np.float32)
    maskb = (j >= t).astype(np.float32)
    scanm = np.ones((128, 512), np.float32)
    scanm[:, 0::128] = 0.0
    B, rc = _pool_tables()
    Bl = np.ascontiguousarray(B.reshape(36, 128, 128).transpose(1, 0, 2)).reshape(128, 36 * 128)
    rcl = np.ascontiguousarray(rc.reshape(1, 12 * 128))
    return ident, maskf, maskb, scanm, Bl, rcl


PAGE = 256
ARENA_BYTES = 204 * 1024
OFF_WA, OFF_WB, OFF_WC, OFF_WORK = 0, 44 * 1024, 88 * 1024, 132 * 1024


class T_:
    def __init__(self, arena, off, nelem, dt):
        self.off, self.nelem, self.dt = off, nelem, dt
        self.esz = 4 if dt == F32 else 2
        assert off % 4 == 0
        nb = (nelem * self.esz + 3) // 4 * 4
        assert off + nb <= ARENA_BYTES, ("arena overflow", off, nb)
        a = arena[:, off // 4:(off + nb) // 4]
        if dt != F32:
            a = a.bitcast(BF16)
        self.ap = a[:, 0:nelem]

    def pg(self, e0=0, n=None):
        if n is None:
            n = self.nelem - e0
        b0 = self.off + e0 * self.esz
        b1 = self.off + (e0 + n) * self.esz - 1
        return [("pg", i) for i in range(b0 // PAGE, b1 // PAGE + 1)]


class Carver:
    def __init__(self, arena, start, end=ARENA_BYTES):
        self.arena, self.off, self.end = arena, start, end

    def tile(self, nelem, dt):
        t = T_(self.arena, self.off, nelem, dt)
        nb = nelem * t.esz
        self.off += (nb + PAGE - 1) // PAGE * PAGE
        assert self.off <= self.end, ("carver overflow", self.off, self.end)
        return t


def build(S, debug=False, phases=(1, 2, 3, 4, 5), ffn_nt=2):
    assert S % 512 == 0
    NCH = S // 128
    NB2 = S // 512
    nc = bass.Bass("TRN2", target_bir_lowering=False)

    def din(name, shape):
        return nc.dram_tensor(name, shape, F32, kind="ExternalInput").ap()

    skind = "ExternalOutput" if debug else "Internal"

    def dscr(name, shape, dt):
        return nc.dram_tensor(name, shape, dt, kind=skind).ap()

    x_d = din("x", [S, D])
    wg_d = [din("wg1", [D, DFF]), din("wg2", [D, DFF])]
    wu_d = [din("wu1", [D, DFF]), din("wu2", [D, DFF])]
    wd_d = [din("wd1", [DFF, D]), din("wd2", [DFF, D])]
    win_d = din("win", [D, 3072])
    wout_d = din("wout", [D, D])
    poolw_d = din("poolw", [4, 128, 128])
    nw_d = din("nw", [4, D])
    lbp_d = din("lbp", [128, 16])
    psc_d = din("psc", [128, 4])
    gn_d = din("gn", [1, 128])
    ident_d = din("c_ident", [128, 128])
    maskf_d = din("c_maskf", [128, 128])
    maskb_d = din("c_maskb", [128, 128])
    scanm_d = din("c_scanm", [128, 512])
    bmat_d = din("c_bmat", [128, 36 * 128])
    rc_d = din("c_rc", [1, 12 * 128])
    out_d = nc.dram_tensor("out", [S, D], F32, kind="ExternalOutput").ap()

    H_d = dscr("H", [S, D], F32)
    REC_d = [dscr("REC0", [NCH, 128, 1536], BF16), dscr("REC1", [NCH, 128, 1536], BF16)]
    VV_d = dscr("VV", [NCH, 128, 512], BF16)
    G_d = dscr("G", [NCH, 128, 512], F32)
    Z_d = dscr("Z", [NCH, 128, 512], BF16)
    OB_d = dscr("OB", [NCH, 128, 512], F32)

    P = Prog()
    es = ExitStack()

    def sb(name, shape, dt):
        return es.enter_context(nc.sbuf_tensor(name, shape, dt))

    arena = sb("arena", [128, ARENA_BYTES // 4], F32)
    ident = sb("ident", [128, 128], BF16)
    expT = sb("expT", [128, 2 * 4 * NCH], F32)
    expTv = expT[:].rearrange("p (d n c) -> p d n c", d=2, n=4)
    EPS_T = sb("eps_t", [128, 8], F32)
    EPS_AP = EPS_T[:]
    stt = sb("stt", [128, 64], F32)
    lbt = sb("lbt", [128, 16], F32)
    lbs = sb("lbs", [128, 32], F32)
    psct = sb("psct", [128, 4], F32)
    banks = [es.enter_context(nc.psum_tensor("bank%d" % i, [128, 512], F32)) for i in range(8)]

    WA = T_(arena, OFF_WA, NK * DFF, BF16)
    WB = T_(arena, OFF_WB, NK * DFF, BF16)
    WC = T_(arena, OFF_WC, NF * D, BF16)
    WAv = WA.ap.rearrange("p (k f) -> p k f", k=NK)
    WBv = WB.ap.rearrange("p (k f) -> p k f", k=NK)
    WCv = WC.ap.rearrange("p (f d) -> p f d", f=NF)

    P.add("pool", lambda e: e.memset(EPS_AP[:, 0:1], EPS), w=[("epsap",)])
    P.add("pool", lambda e: e.memset(EPS_AP[:, 1:8], -0.5), w=[("epsap",)])
    P.add("pool", lambda e: e.dma_start(out=ident[:], in_=ident_d), w=[("ident",)], dma="ident")

    def BK(i):
        return [("bank", i)]

    def load_ffn_w(li, which):
        if "g" in which:
            for (T, dstv, src, nm) in ((WA, WAv, wg_d[li], "WA"), (WB, WBv, wu_d[li], "WB")):
                srcv = src.rearrange("(k p) f -> p k f", p=128)
                for pc in range(4):
                    c0, c1 = pc * 704, (pc + 1) * 704
                    wl = []
                    for k in range(NK):
                        wl += T.pg(k * DFF + c0, 704)
                    P.add("pool", lambda e, dstv=dstv, srcv=srcv, c0=c0, c1=c1: e.dma_start(out=dstv[:, :, c0:c1], in_=srcv[:, :, c0:c1]),
                          w=wl, dma=(nm, pc))
        if "d" in which:
            srcv = wd_d[li].rearrange("(f p) d -> p f d", p=128)
            for pc in range(2):
                f0, f1 = pc * 11, (pc + 1) * 11
                P.add("pool", lambda e, srcv=srcv, f0=f0, f1=f1: e.dma_start(out=WCv[:, f0:f1, :], in_=srcv[:, f0:f1, :]),
                      w=WC.pg(f0 * D, 11 * D), dma=("WC", pc))

    def rms_ops(tagp, col, x_ap, x_pg, junk_ap, junk_pg, out_ap, out_pg, nwrow, n=D):
        sq = stt[:, col:col + 1]
        rs = stt[:, 32 + col:33 + col]
        k1, k2 = ("st", col), ("st", 32 + col)
        P.add("act", lambda e: e.activation(out=junk_ap, in_=x_ap, func=AF.Square, accum_out=sq), r=x_pg, w=junk_pg + [k1])
        P.add("pool", lambda e: e.tensor_scalar(out=rs, in0=sq, scalar1=1.0 / n, scalar2=EPS, op0=ALU.mult, op1=ALU.add),
              r=[k1], w=[k2])
        P.add("pool", lambda e: e.tensor_tensor(out=rs, in0=rs, in1=EPS_AP[:, 1:2], op=ALU.pow), r=[k2, ("epsap",)], w=[k2])
        P.add("dve", lambda e: e.scalar_tensor_tensor(out=out_ap, in0=x_ap, scalar=rs, in1=nwrow, op0=ALU.mult, op1=ALU.mult),
              r=x_pg + [k2, ("nwt",)], w=out_pg)

    def ffn_phase(tagp, src_d, src_nm, dst_d, dst_nm, nw_i, final_i, stcol):
        cv = Carver(arena, OFF_WORK)
        nwt = cv.tile(2 * D, F32)
        nwv = nwt.ap.rearrange("p (a d) -> p a d", a=2)
        P.add("sp", lambda e: e.dma_start(out=nwv[:, 0, :], in_=nw_d[nw_i:nw_i + 1, :].rearrange("a d -> (a d)").partition_broadcast(128)),
              w=nwt.pg(0, D) + [("nwt",)], dma=("nwt", tagp, 0))
        if final_i is not None:
            P.add("sp", lambda e: e.dma_start(out=nwv[:, 1, :], in_=nw_d[final_i:final_i + 1, :].rearrange("a d -> (a d)").partition_broadcast(128)),
                  w=nwt.pg(D, D) + [("nwt",)], dma=("nwt", tagp, 1))
        NX = 3 * ffn_nt
        xt = [cv.tile(D, F32) for _ in range(NX)]
        xn = [cv.tile(D, BF16) for _ in range(2)]
        xnT = [cv.tile(NK * 128 * ffn_nt, BF16) for _ in range(2)]
        hT = cv.tile(NF * 128 * ffn_nt, BF16)
        sg = [cv.tile(128 * ffn_nt, BF16) for _ in range(2)]
        TW = 128 * ffn_nt
        blocks = []
        t0 = 0
        ntiles = S // 128
        while t0 < ntiles:
            n = min(ffn_nt, ntiles - t0)
            blocks.append((t0, n))
            t0 += n
        gtile = [0]

        def do_load(b):
            tb, n = blocks[b]
            for i in range(n):
                ti = tb + i
                xa = xt[ti % NX]
                P.add("sp", lambda e, xa=xa, ti=ti: e.dma_start(out=xa.ap, in_=src_d[ti * 128:(ti + 1) * 128, :]),
                      r=[("dram", src_nm, ti)], w=xa.pg(), dma=("xt", tagp, ti % NX))

        def do_norm(b):
            tb, n = blocks[b]
            for i in range(n):
                ti = tb + i
                xa = xt[ti % NX]
                xs = xn[ti % 2]
                rms_ops(tagp, stcol + ti % 4, xa.ap, xa.pg(), xs.ap, xs.pg(), xs.ap, xs.pg(), nwv[:, 0, :])

        def do_transpose(b):
            tb, n = blocks[b]
            xT = xnT[b % 2]
            xTv = xT.ap.rearrange("p (k t) -> p k t", k=NK)
            for i in range(n):
                ti = tb + i
                xs = xn[ti % 2]
                bk = 6 + (ti % 2)
                pv = banks[bk][:].bitcast(BF16).rearrange("p (k t) -> p k t", k=NK)
                for k in range(NK):
                    P.add("pe", lambda e, pv=pv, xs=xs, k=k: e.transpose(out=pv[:, k, :], in_=xs.ap[:, k * 128:(k + 1) * 128], identity=ident[:]),
                          r=xs.pg(k * 128, 128) + [("ident",)], w=BK(bk))
                wl = []
                for k in range(NK):
                    wl += xT.pg(k * TW + i * 128, 128)
                P.add("act", lambda e, pv=pv, xTv=xTv, i=i: e.activation(out=xTv[:, :, i * 128:(i + 1) * 128], in_=pv, func=AF.Copy),
                      r=BK(bk), w=wl)

        def do_gateup(b, mid_hook):
            tb, n = blocks[b]
            ntok = n * 128
            xT = xnT[b % 2]
            xTv = xT.ap.rearrange("p (k t) -> p k t", k=NK)
            hTv = hT.ap.rearrange("p (f t) -> p f t", f=NF)
            for f in range(NF):
                if f == NF // 2 and mid_hook is not None:
                    mid_hook()
                pg_, pu_ = banks[f % 2], banks[2 + f % 2]
                for (W, Wv, pp, bki) in ((WA, WAv, pg_, f % 2), (WB, WBv, pu_, 2 + f % 2)):
                    for k in range(NK):
                        P.add("pe", lambda e, pp=pp, Wv=Wv, k=k, f=f, xTv=xTv, ntok=ntok: e.matmul(
                            pp[:, 0:ntok], lhsT=Wv[:, k, f * 128:(f + 1) * 128], rhs=xTv[:, k, 0:ntok], start=(k == 0), stop=(k == NK - 1)),
                            r=xT.pg(k * TW, ntok) + W.pg(k * DFF + f * 128, 128), w=BK(bki))
                sgt = sg[f % 2]
                P.add("act", lambda e, sgt=sgt, pg_=pg_, ntok=ntok: e.activation(out=sgt.ap[:, 0:ntok], in_=pg_[:, 0:ntok], func=AF.Silu),
                      r=BK(f % 2), w=sgt.pg())
                P.add("dve", lambda e, sgt=sgt, pu_=pu_, f=f, hTv=hTv, ntok=ntok: e.tensor_tensor(
                    out=hTv[:, f, 0:ntok], in0=pu_[:, 0:ntok], in1=sgt.ap[:, 0:ntok], op=ALU.mult),
                    r=sgt.pg() + BK(2 + f % 2), w=hT.pg(f * TW, ntok))

        def do_down(b):
            tb, n = blocks[b]
            hTv = hT.ap.rearrange("p (f t) -> p f t", f=NF)
            for i in range(n):
                ti = tb + i
                xa = xt[ti % NX]
                for half in range(2):
                    j = gtile[0]
                    gtile[0] += 1
                    bk = 4 + j % 2
                    pd_ = banks[bk]
                    for f in range(NF):
                        P.add("pe", lambda e, pd_=pd_, f=f, i=i, half=half: e.matmul(
                            pd_[:, :], lhsT=hTv[:, f, i * 128:(i + 1) * 128], rhs=WCv[:, f, half * 512:(half + 1) * 512],
                            start=(f == 0), stop=(f == NF - 1)),
                            r=hT.pg(f * TW + i * 128, 128) + WC.pg(f * D + half * 512, 512), w=BK(bk))
                    xh = xa.ap[:, half * 512:(half + 1) * 512]
                    P.add("dve", lambda e, pd_=pd_, xh=xh: e.scalar_tensor_tensor(
                        out=xh, in0=pd_[:, :], scalar=0.5, in1=xh, op0=ALU.mult, op1=ALU.add),
                        r=BK(bk) + xa.pg(half * 512, 512), w=xa.pg(half * 512, 512))
                if final_i is not None:
                    xs = xn[ti % 2]
                    rms_ops(tagp, stcol + 4 + ti % 4, xa.ap, xa.pg(), xs.ap, xs.pg(), xa.ap, xa.pg(), nwv[:, 1, :])
                P.add("sp", lambda e, xa=xa, ti=ti: e.dma_start(out=dst_d[ti * 128:(ti + 1) * 128, :], in_=xa.ap),
                      r=xa.pg(), w=[("dram", dst_nm, ti)], dma=("xt", tagp, ti % NX))

        nb = len(blocks)
        do_load(0)
        if nb > 1:
            do_load(1)
        do_norm(0)
        do_transpose(0)
        for b in range(nb):
            if b + 2 < nb:
                do_load(b + 2)
            hook = (lambda b=b: do_norm(b + 1)) if b + 1 < nb else None
            do_gateup(b, hook)
            if b + 1 < nb:
                do_transpose(b + 1)
            do_down(b)

    def phase2():
        cv = Carver(arena, 0)
        WIN = cv.tile(NK * 3072, BF16)
        WINv = WIN.ap.rearrange("p (k c) -> p k c", k=NK)
        srcv = win_d.rearrange("(k p) c -> p k c", p=128)
        for pc in range(6):
            wl = []
            for k in range(NK):
                wl += WIN.pg(k * 3072 + pc * 512, 512)
            P.add("pool", lambda e, pc=pc: e.dma_start(out=WINv[:, :, pc * 512:(pc + 1) * 512], in_=srcv[:, :, pc * 512:(pc + 1) * 512]),
                  w=wl, dma=("WIN", pc))
        nwt = cv.tile(D, F32)
        P.add("sp", lambda e: e.dma_start(out=nwt.ap, in_=nw_d[1:2, :].rearrange("a d -> (a d)").partition_broadcast(128)),
              w=nwt.pg() + [("nwt",)], dma=("nwt", "p2", 0))
        gnb = cv.tile(512, F32)
        gnbv = gnb.ap.rearrange("p (n v) -> p n v", n=4)
        for n in range(4):
            P.add("sp", lambda e, n=n: e.dma_start(out=gnbv[:, n, :], in_=gn_d.rearrange("a d -> (a d)").partition_broadcast(128)),
                  w=gnb.pg(n * 128, 128), dma=("gnb", n))
        scm = cv.tile(512, F32)
        P.add("sp", lambda e: e.dma_start(out=scm.ap, in_=scanm_d), w=scm.pg(), dma=("scm",))
        P.add("sp", lambda e: e.dma_start(out=lbt[:], in_=lbp_d), w=[("lbt",)], dma=("lbt",))
        lbtv = lbt[:].rearrange("p (d l n) -> p d l n", d=2, l=2)
        lbv = lbs[:, 0:8].rearrange("p (d n) -> p d n", d=2)
        P.add("dve", lambda e: e.tensor_tensor(out=lbv, in0=lbtv[:, :, 0, :], in1=lbtv[:, :, 1, :], op=ALU.subtract), r=[("lbt",)], w=[("lbs", 0)])
        P.add("act", lambda e: e.activation(out=lbs[:, 0:8], in_=lbs[:, 0:8], func=AF.Sigmoid), r=[("lbs", 0)], w=[("lbs", 0)])
        P.add("dve", lambda e: e.tensor_scalar(out=lbs[:, 8:16], in0=lbs[:, 0:8], scalar1=-1.0, scalar2=1.0, op0=ALU.mult, op1=ALU.add),
              r=[("lbs", 0)], w=[("lbs", 1)])
        P.add("dve", lambda e: e.tensor_scalar(out=lbs[:, 16:24], in0=lbs[:, 0:8], scalar1=1.0, scalar2=-1.0, op0=ALU.mult, op1=ALU.add),
              r=[("lbs", 0)], w=[("lbs", 2)])
        LBK = [("lbs", 0), ("lbs", 1), ("lbs", 2)]

        ht = [cv.tile(D, F32) for _ in range(4)]
        xn = [cv.tile(D, BF16) for _ in range(2)]
        uT = [cv.tile(NK * 512, BF16) for _ in range(1)]
        sig = [[cv.tile(512, F32) for _ in range(8)] for _ in range(2)]
        tq = [[cv.tile(512, F32) for _ in range(3)] for _ in range(4)]
        qs = [[cv.tile(512, F32) for _ in range(4)] for _ in range(2)]
        RST = [cv.tile(4 * 1536, BF16) for _ in range(2)]
        RSTv = [r.ap.rearrange("p (c k n t) -> p c k n t", c=4, k=3, n=4) for r in RST]
        GST = cv.tile(4 * 512, F32)
        ZST = cv.tile(4 * 512, BF16)
        VST = cv.tile(4 * 512, BF16)
        gtmp = [cv.tile(512, F32) for _ in range(2)]
        NHT = 4

        def rst_pg(d, c, kk_, n):
            return RST[d].pg(((c * 3 + kk_) * 4 + n) * 128, 128)

        def load_h(b):
            for i in range(4):
                ti = b * 4 + i
                xa = ht[ti % NHT]
                P.add("sp", lambda e, xa=xa, ti=ti: e.dma_start(out=xa.ap, in_=H_d[ti * 128:(ti + 1) * 128, :]),
                      r=[("dram", "H", ti)], w=xa.pg(), dma=("ht", ti % NHT))

        cnt = [0]

        def nb_():
            cnt[0] += 1
            return cnt[0]

        u = uT[0]
        uv = u.ap.rearrange("p (k t) -> p k t", k=NK)

        def A1(b):
            for i in range(4):
                ti = b * 4 + i
                xa = ht[ti % NHT]
                xs = xn[ti % 2]
                rms_ops("p2", 8 + ti % 4, xa.ap, xa.pg(), xs.ap, xs.pg(), xs.ap, xs.pg(), nwt.ap)
                bk = 6 + (ti % 2)
                pv = banks[bk][:].bitcast(BF16).rearrange("p (k t) -> p k t", k=NK)
                for k in range(NK):
                    P.add("pe", lambda e, pv=pv, xs=xs, k=k: e.transpose(out=pv[:, k, :], in_=xs.ap[:, k * 128:(k + 1) * 128], identity=ident[:]),
                          r=xs.pg(k * 128, 128) + [("ident",)], w=BK(bk))
                wl = []
                for k in range(NK):
                    wl += u.pg(k * 512 + i * 128, 128)
                P.add("act", lambda e, pv=pv, i=i: e.activation(out=uv[:, :, i * 128:(i + 1) * 128], in_=pv, func=AF.Copy),
                      r=BK(bk), w=wl)
            if b + 1 < NB2:
                load_h(b + 1)
            for (c0, kind) in ((0, "z"), (1024, "v"), (2560, "g")):
                for i in range(4):
                    bk = nb_() % 2
                    pp = banks[bk]
                    for k in range(NK):
                        P.add("pe", lambda e, pp=pp, k=k, i=i, c0=c0: e.matmul(
                            pp[:, :], lhsT=uv[:, k, i * 128:(i + 1) * 128], rhs=WINv[:, k, c0:c0 + 512], start=(k == 0), stop=(k == NK - 1)),
                            r=u.pg(k * 512 + i * 128, 128) + WIN.pg(k * 3072 + c0, 512), w=BK(bk))
                    if kind == "z":
                        P.add("act", lambda e, pp=pp, i=i: e.activation(out=ZST.ap[:, i * 512:(i + 1) * 512], in_=pp[:, :], func=AF.Copy),
                              r=BK(bk), w=ZST.pg(i * 512, 512))
                    elif kind == "v":
                        P.add("act", lambda e, pp=pp, i=i: e.activation(out=VST.ap[:, i * 512:(i + 1) * 512], in_=pp[:, :], func=AF.Copy),
                              r=BK(bk), w=VST.pg(i * 512, 512))
                    else:
                        gt = gtmp[i % 2]
                        P.add("act", lambda e, pp=pp, gt=gt: e.activation(out=gt.ap, in_=pp[:, :], func=AF.Silu), r=BK(bk), w=gt.pg())
                        P.add("pool", lambda e, gt=gt, i=i: e.tensor_tensor(out=GST.ap[:, i * 512:(i + 1) * 512], in0=gt.ap, in1=gnb.ap, op=ALU.mult),
                              r=gt.pg() + gnb.pg(), w=GST.pg(i * 512, 512))
            P.add("sp", lambda e, b=b: e.dma_start(out=VV_d[b * 4:(b + 1) * 4].rearrange("c p f -> p c f"), in_=VST.ap.rearrange("p (c f) -> p c f", c=4)),
                  r=VST.pg(), w=[("dram", "VV", b)], dma=("VST",))
            P.add("sp", lambda e, b=b: e.dma_start(out=Z_d[b * 4:(b + 1) * 4].rearrange("c p f -> p c f"), in_=ZST.ap.rearrange("p (c f) -> p c f", c=4)),
                  r=ZST.pg(), w=[("dram", "Z", b)], dma=("ZST",))
            P.add("sp", lambda e, b=b: e.dma_start(out=G_d[b * 4:(b + 1) * 4].rearrange("c p f -> p c f"), in_=GST.ap.rearrange("p (c f) -> p c f", c=4)),
                  r=GST.pg(), w=[("dram", "G", b)], dma=("GST",))

        def A2(b):
            for n in range(4):
                bk = 2 + nb_() % 3
                pp = banks[bk]
                c0 = 512 + n * 128
                qt = qs[b % 2][n]
                for k in range(NK):
                    P.add("pe", lambda e, pp=pp, k=k, c0=c0: e.matmul(
                        pp[:, :], lhsT=WINv[:, k, c0:c0 + 128], rhs=uv[:, k, :], start=(k == 0), stop=(k == NK - 1)),
                        r=u.pg(k * 512, 512) + WIN.pg(k * 3072 + c0, 128), w=BK(bk))
                P.add("act", lambda e, pp=pp, qt=qt: e.activation(out=qt.ap, in_=pp[:, :], func=AF.Silu), r=BK(bk), w=qt.pg())
            for hd in range(8):
                bk = 2 + nb_() % 3
                pp = banks[bk]
                c0 = 1536 + hd * 128
                t0_ = sig[b % 2][hd]
                for k in range(NK):
                    P.add("pe", lambda e, pp=pp, k=k, c0=c0: e.matmul(
                        pp[:, :], lhsT=WINv[:, k, c0:c0 + 128], rhs=uv[:, k, :], start=(k == 0), stop=(k == NK - 1)),
                        r=u.pg(k * 512, 512) + WIN.pg(k * 3072 + c0, 128), w=BK(bk))
                P.add("act", lambda e, pp=pp, t0_=t0_: e.activation(out=t0_.ap, in_=pp[:, :], func=AF.Sigmoid), r=BK(bk), w=t0_.pg())

        def Bchain(b, d):
            hds = [(d * 4 + n, n) for n in range(4)]
            for hd, n in hds:
                t0_ = sig[b % 2][hd]
                t1_, t2_, t3_ = tq[n]
                P.add("dve", lambda e, t0_=t0_, t1_=t1_, hd=hd: e.tensor_scalar(
                    out=t1_.ap, in0=t0_.ap, scalar1=lbs[:, 8 + hd:9 + hd], scalar2=lbs[:, hd:hd + 1], op0=ALU.mult, op1=ALU.add),
                    r=t0_.pg() + LBK, w=t1_.pg())
                P.add("dve", lambda e, t0_=t0_, t2_=t2_, hd=hd: e.tensor_scalar(
                    out=t2_.ap, in0=t0_.ap, scalar1=lbs[:, 16 + hd:17 + hd], scalar2=lbs[:, 8 + hd:9 + hd], op0=ALU.mult, op1=ALU.add),
                    r=t0_.pg() + LBK, w=t2_.pg())
            for hd, n in hds:
                t1_ = tq[n][0]
                P.add("act", lambda e, t1_=t1_: e.activation(out=t1_.ap, in_=t1_.ap, func=AF.Ln), r=t1_.pg(), w=t1_.pg())
            for hd, n in hds:
                t0_ = sig[b % 2][hd]
                t1_, t2_, t3_ = tq[n]
                P.add("dve", lambda e, t1_=t1_, t3_=t3_: e.tensor_tensor_scan(
                    out=t3_.ap, data0=scm.ap, data1=t1_.ap, initial=0.0, op0=ALU.mult, op1=ALU.add),
                    r=t1_.pg() + scm.pg(), w=t3_.pg())
                c3 = t3_.ap.rearrange("p (c t) -> p c t", c=4)
                a3 = t0_.ap.rearrange("p (c t) -> p c t", c=4)
                if d == 0:
                    P.add("dve", lambda e, c3=c3, a3=a3: e.tensor_tensor(
                        out=a3, in0=c3, in1=c3[:, :, 127:128].to_broadcast([128, 4, 128]), op=ALU.subtract),
                        r=t3_.pg(), w=t0_.pg())
                else:
                    P.add("dve", lambda e, t0_=t0_, t1_=t1_, t3_=t3_: e.tensor_tensor(out=t0_.ap, in0=t1_.ap, in1=t3_.ap, op=ALU.subtract),
                          r=t1_.pg() + t3_.pg(), w=t0_.pg())
            for hd, n in hds:
                t0_ = sig[b % 2][hd]
                t1_, t2_, t3_ = tq[n]
                c3 = t3_.ap.rearrange("p (c t) -> p c t", c=4)
                P.add("act", lambda e, c3=c3, n=n: e.activation(out=expTv[:, d, n, b * 4:(b + 1) * 4], in_=c3[:, :, 127], func=AF.Exp),
                      r=t3_.pg(), w=[("expT", d, n, b)])
                P.add("act", lambda e, t0_=t0_, t1_=t1_: e.activation(out=t1_.ap, in_=t0_.ap, func=AF.Exp), r=t0_.pg(), w=t1_.pg())
                P.add("act", lambda e, t0_=t0_, t3_=t3_: e.activation(out=t3_.ap, in_=t0_.ap, func=AF.Exp, scale=-1.0), r=t0_.pg(), w=t3_.pg())
            for hd, n in hds:
                t1_, t2_, t3_ = tq[n]
                qt = qs[b % 2][n]
                wl0, wl1 = [], []
                for c in range(4):
                    wl0 += rst_pg(d, c, 0, n)
                    wl1 += rst_pg(d, c, 1, n)
                P.add("dve", lambda e, n=n, t1_=t1_, qt=qt: e.tensor_tensor(
                    out=RSTv[d][:, :, 0, n, :], in0=qt.ap.rearrange("p (c t) -> p c t", c=4), in1=t1_.ap.rearrange("p (c t) -> p c t", c=4), op=ALU.mult),
                    r=qt.pg() + t1_.pg(), w=wl0)
                P.add("pool", lambda e, n=n, t2_=t2_, t3_=t3_: e.tensor_tensor(
                    out=RSTv[d][:, :, 1, n, :], in0=t2_.ap.rearrange("p (c t) -> p c t", c=4), in1=t3_.ap.rearrange("p (c t) -> p c t", c=4), op=ALU.mult),
                    r=t2_.pg() + t3_.pg(), w=wl1)

        def BT(b, d):
            for n in range(4):
                bk = 6 + n % 2
                pv = banks[bk][:].bitcast(BF16)[:, 0:512].rearrange("p (c t) -> p c t", c=4)
                for c in range(4):
                    P.add("pe", lambda e, pv=pv, n=n, c=c: e.transpose(out=pv[:, c, :], in_=RSTv[d][:, c, 1, n, :], identity=ident[:]),
                          r=rst_pg(d, c, 1, n) + [("ident",)], w=BK(bk))
                wl2 = []
                for c in range(4):
                    wl2 += rst_pg(d, c, 2, n)
                P.add("act", lambda e, pv=pv, n=n: e.activation(out=RSTv[d][:, :, 2, n, :], in_=pv, func=AF.Copy), r=BK(bk), w=wl2)
            P.add("sp", lambda e: e.dma_start(
                out=REC_d[d][b * 4:(b + 1) * 4].rearrange("c p f -> p c f"), in_=RST[d].ap.rearrange("p (c f) -> p c f", c=4)),
                r=RST[d].pg(), w=[("dram", "REC%d" % d, b)], dma=("RST", d))

        load_h(0)
        A1(0)
        A2(0)
        for b in range(NB2):
            Bchain(b, 0)
            if b + 1 < NB2:
                A1(b + 1)
            BT(b, 0)
            Bchain(b, 1)
            if b + 1 < NB2:
                A2(b + 1)
            BT(b, 1)

    def phase3():
        cv = Carver(arena, OFF_WC)
        WO = cv.tile(NK * D, BF16)
        WOv = WO.ap.rearrange("p (k d) -> p k d", k=NK)
        P.add("pool", lambda e: e.dma_start(out=WOv, in_=wout_d.rearrange("(k p) d -> p k d", p=128)), w=WO.pg(), dma=("WO",))
        BM = cv.tile(36 * 128, BF16)
        BMv = BM.ap.rearrange("p (m t) -> p m t", m=36)
        P.add("pool", lambda e: e.dma_start(out=BM.ap, in_=bmat_d), w=BM.pg(), dma=("BM",))
        RCB = cv.tile(12 * 128, F32)
        RCv = RCB.ap.rearrange("p (v g t) -> p v g t", v=3, g=4)
        P.add("sp", lambda e: e.dma_start(out=RCB.ap, in_=rc_d.rearrange("a d -> (a d)").partition_broadcast(128)), w=RCB.pg(), dma=("RCB",))
        MK = [cv.tile(128, F32), cv.tile(128, F32)]
        P.add("sp", lambda e: e.dma_start(out=MK[0].ap, in_=maskf_d), w=MK[0].pg(), dma=("MK", 0))
        P.add("sp", lambda e: e.dma_start(out=MK[1].ap, in_=maskb_d), w=MK[1].pg(), dma=("MK", 1))
        PW = cv.tile(512, BF16)
        PWv = PW.ap.rearrange("p (g d) -> p g d", g=4)
        P.add("pool", lambda e: e.dma_start(out=PWv, in_=poolw_d.rearrange("g c d -> c g d")), w=PW.pg(), dma=("PW",))
        P.add("sp", lambda e: e.dma_start(out=psct[:], in_=psc_d), w=[("psct",)], dma=("psct",))
        St = cv.tile(512, F32)
        S1 = cv.tile(512, F32)
        S1b = cv.tile(512, BF16)
        Sv = St.ap.rearrange("p (n v) -> p n v", n=4)
        S1v = S1.ap.rearrange("p (n v) -> p n v", n=4)
        S1bv = S1b.ap.rearrange("p (n v) -> p n v", n=4)
        NR = 4
        rec = [cv.tile(1536, BF16) for _ in range(NR)]
        vv = [cv.tile(512, BF16) for _ in range(NR)]
        PT = [cv.tile(512, BF16) for _ in range(2)]
        obl = [cv.tile(512, F32) for _ in range(3)]
        obst = obl[0:2]
        gl = [cv.tile(512, F32) for _ in range(4)]
        zl = [cv.tile(3 * 512, BF16) for _ in range(3)]
        hl = [cv.tile(D, F32) for _ in range(4)]
        ot = [cv.tile(512, F32) for _ in range(2)]
        junk = cv.tile(512, BF16)
        mix = [cv.tile(512, BF16) for _ in range(2)]
        mixT = [cv.tile(NK * 128, BF16) for _ in range(3)]
        pooledT = [cv.tile(512, BF16) for _ in range(2)]

        def run_dir(d):
            order = list(range(NCH)) if d == 0 else list(range(NCH - 1, -1, -1))
            fwd = (d == 0)
            P.add("pool", lambda e: e.memset(St.ap, 0.0), w=St.pg())
            bA = 0
            bSs = (1, 2)
            bO = 3

            def ok(i):
                return 0 <= i < NCH

            def ld_rec(i):
                c = order[i]
                sl = i % NR
                P.add("sp", lambda e, c=c, sl=sl: e.dma_start(out=rec[sl].ap, in_=REC_d[d][c]),
                      r=[("dram", "REC%d" % d, c // 4)], w=rec[sl].pg(), dma=("rec", sl))
                P.add("sp", lambda e, c=c, sl=sl: e.dma_start(out=vv[sl].ap, in_=VV_d[c]),
                      r=[("dram", "VV", c // 4)], w=vv[sl].pg(), dma=("vv", sl))

            def ld_b(i):
                c = order[i]
                s3 = i % 3
                s4 = i % 4
                P.add("sp", lambda e, c=c, s3=s3: e.dma_start(out=obl[s3].ap, in_=OB_d[c]), r=[("dram", "OB", c)], w=obl[s3].pg(), dma=("obl", s3))
                P.add("sp", lambda e, c=c, s4=s4: e.dma_start(out=gl[s4].ap, in_=G_d[c]), r=[("dram", "G", c // 4)], w=gl[s4].pg(), dma=("gl", s4))
                lo, hi = max(c - 1, 0), min(c + 1, NCH - 1)
                zv = zl[s3].ap.rearrange("p (j f) -> p j f", j=3)
                P.add("sp", lambda e, lo=lo, hi=hi, c=c, zv=zv: e.dma_start(
                    out=zv[:, lo - c + 1:hi - c + 2, :], in_=Z_d[lo:hi + 1].rearrange("c p f -> p c f")),
                    r=[("dram", "Z", lo // 4), ("dram", "Z", hi // 4)], w=zl[s3].pg(), dma=("zl", s3))

            def ld_h(i):
                c = order[i]
                s4 = i % 4
                P.add("sp", lambda e, c=c, s4=s4: e.dma_start(out=hl[s4].ap, in_=H_d[c * 128:(c + 1) * 128, :]),
                      r=[("dram", "H", c)], w=hl[s4].pg(), dma=("hl", s4))

            def views(i):
                sl = i % NR
                rv = rec[sl].ap.rearrange("p (k n t) -> p k n t", k=3, n=4)
                vvv = vv[sl].ap.rearrange("p (n v) -> p n v", n=4)
                return sl, rv, vvv

            def A1_pe(i):
                sl, rv, vvv = views(i)
                pA = banks[bA][:].rearrange("p (n t) -> p n t", n=4)
                bS = bSs[i % 2]
                pS = banks[bS][:].rearrange("p (n t) -> p n t", n=4)
                for n in range(4):
                    P.add("pe", lambda e, pA=pA, rv=rv, n=n: e.matmul(pA[:, n, :], lhsT=rv[:, 1, n, :], rhs=rv[:, 0, n, :], start=True, stop=True),
                          r=rec[sl].pg(), w=BK(bA))
                for n in range(4):
                    P.add("pe", lambda e, pS=pS, rv=rv, vvv=vvv, n=n: e.matmul(pS[:, n, :], lhsT=rv[:, 2, n, :], rhs=vvv[:, n, :], start=True, stop=True),
                          r=rec[sl].pg() + vv[sl].pg(), w=BK(bS))

            def A1_dve(i):
                pA = banks[bA][:].rearrange("p (n t) -> p n t", n=4)
                pt = PT[i % 2]
                ptv = pt.ap.rearrange("p (n t) -> p n t", n=4)
                P.add("dve", lambda e, pA=pA, ptv=ptv: e.tensor_tensor(
                    out=ptv, in0=pA, in1=MK[d].ap.unsqueeze(1).to_broadcast([128, 4, 128]), op=ALU.mult),
                    r=BK(bA) + MK[d].pg(), w=pt.pg())

            def A2_s1(i):
                c = order[i]
                P.add("dve", lambda e, c=c: e.tensor_tensor(out=S1v, in0=Sv, in1=expTv[:, d, :, c:c + 1].to_broadcast([128, 4, 128]), op=ALU.mult),
                      r=St.pg() + [("expT", d, n, c // 4) for n in range(4)], w=S1.pg())
                P.add("act", lambda e: e.activation(out=S1b.ap, in_=S1.ap, func=AF.Copy), r=S1.pg(), w=S1b.pg())

            def A2_upd(i):
                bS = bSs[i % 2]
                pS = banks[bS][:].rearrange("p (n t) -> p n t", n=4)
                P.add("dve", lambda e, pS=pS: e.tensor_tensor(out=Sv, in0=S1v, in1=pS, op=ALU.add), r=S1.pg() + BK(bS), w=St.pg())

            def A2_pso(i):
                c = order[i]
                sl, rv, vvv = views(i)
                pO = banks[bO][:].rearrange("p (n t) -> p n t", n=4)
                pt = PT[i % 2]
                ptv = pt.ap.rearrange("p (n t) -> p n t", n=4)
                for n in range(4):
                    P.add("pe", lambda e, pO=pO, rv=rv, n=n: e.matmul(pO[:, n, :], lhsT=rv[:, 0, n, :], rhs=S1bv[:, n, :], start=True, stop=False),
                          r=rec[sl].pg() + S1b.pg(), w=BK(bO))
                    P.add("pe", lambda e, pO=pO, ptv=ptv, vvv=vvv, n=n: e.matmul(pO[:, n, :], lhsT=ptv[:, n, :], rhs=vvv[:, n, :], start=False, stop=True),
                          r=pt.pg() + vv[sl].pg(), w=BK(bO))
                if not fwd:
                    ob = obst[i % 2]
                    P.add("act", lambda e, ob=ob: e.activation(out=ob.ap, in_=banks[bO][:, :], func=AF.Copy), r=BK(bO), w=ob.pg())
                    P.add("sp", lambda e, ob=ob, c=c: e.dma_start(out=OB_d[c], in_=ob.ap), r=ob.pg(), w=[("dram", "OB", c)], dma=("obl", i % 2))

            def B1(i):
                s3 = i % 3
                o_ = ot[i % 2]
                sc = 16 + 4 * (i % 2)
                rc_ = 48 + 4 * (i % 2)
                P.add("dve", lambda e, s3=s3, o_=o_: e.tensor_tensor(out=o_.ap, in0=banks[bO][:, :], in1=obl[s3].ap, op=ALU.add),
                      r=BK(bO) + obl[s3].pg(), w=o_.pg())
                for n in range(4):
                    P.add("act", lambda e, n=n, o_=o_, sc=sc: e.activation(out=junk.ap[:, n * 128:(n + 1) * 128], in_=o_.ap[:, n * 128:(n + 1) * 128],
                                                                           func=AF.Square, accum_out=stt[:, sc + n:sc + n + 1]),
                          r=o_.pg(n * 128, 128), w=junk.pg(n * 128, 128) + [("st", sc + n)])

            def B2(i):
                o_ = ot[i % 2]
                rc_ = 48 + 4 * (i % 2)
                s4 = i % 4
                otv = o_.ap.rearrange("p (n v) -> p n v", n=4)
                mx = mix[i % 2]
                sc = 16 + 4 * (i % 2)
                P.add("pool", lambda e, sc=sc, rc_=rc_: e.tensor_scalar(out=stt[:, rc_:rc_ + 4], in0=stt[:, sc:sc + 4], scalar1=1.0 / 128, scalar2=EPS,
                                                                       op0=ALU.mult, op1=ALU.add),
                      r=[("st", sc + n) for n in range(4)], w=[("st", rc_)])
                P.add("pool", lambda e, rc_=rc_: e.tensor_tensor(out=stt[:, rc_:rc_ + 4], in0=stt[:, rc_:rc_ + 4], in1=EPS_AP[:, 1:5], op=ALU.pow),
                      r=[("st", rc_), ("epsap",)], w=[("st", rc_)])
                P.add("pool", lambda e, otv=otv, rc_=rc_: e.tensor_tensor(out=otv, in0=otv, in1=stt[:, rc_:rc_ + 4].unsqueeze(2).to_broadcast([128, 4, 128]), op=ALU.mult),
                      r=o_.pg() + [("st", rc_)], w=o_.pg())
                P.add("pool", lambda e, s4=s4, mx=mx, o_=o_: e.tensor_tensor(out=mx.ap, in0=o_.ap, in1=gl[s4].ap, op=ALU.mult),
                      r=o_.pg() + gl[s4].pg(), w=mx.pg())

            def PP1(i):
                c = order[i]
                s3 = i % 3
                var = 0 if c == 0 else (2 if c == NCH - 1 else 1)
                zv = zl[s3].ap.rearrange("p (j f) -> p j f", j=3)
                pPv = banks[4][:].rearrange("p (g t) -> p g t", g=4)
                poss = [p_ for p_ in range(3) if 0 <= c + p_ - 1 < NCH]
                for g in range(4):
                    for q_, p_ in enumerate(poss):
                        P.add("pe", lambda e, g=g, p_=p_, q_=q_, zv=zv, pPv=pPv, var=var, poss=poss: e.matmul(
                            pPv[:, g, :], lhsT=zv[:, p_, g * 128:(g + 1) * 128], rhs=BMv[:, (var * 3 + p_) * 4 + g, :],
                            start=(q_ == 0), stop=(q_ == len(poss) - 1)),
                            r=zl[s3].pg() + BM.pg(), w=BK(4))
                pl = pooledT[i % 2]
                P.add("dve", lambda e, pPv=pPv, var=var, pl=pl: e.tensor_tensor(
                    out=pl.ap.rearrange("p (g t) -> p g t", g=4), in0=pPv, in1=RCv[:, var, :, :], op=ALU.mult),
                    r=BK(4) + RCB.pg(), w=pl.pg())

            def PP2(i):
                pl = pooledT[i % 2]
                mt = mixT[i % 3]
                mtv = mt.ap.rearrange("p (k t) -> p k t", k=NK)
                pYv = banks[5][:].rearrange("p (g t) -> p g t", g=4)
                plv = pl.ap.rearrange("p (g t) -> p g t", g=4)
                for g in range(4):
                    P.add("pe", lambda e, g=g, pYv=pYv, plv=plv: e.matmul(pYv[:, g, :], lhsT=PWv[:, g, :], rhs=plv[:, g, :], start=True, stop=True),
                          r=pl.pg() + PW.pg(), w=BK(5))
                P.add("dve", lambda e, pYv=pYv, mtv=mtv: e.tensor_tensor(
                    out=mtv[:, 0:4, :], in0=pYv, in1=psct[:, 0:4].unsqueeze(2).to_broadcast([128, 4, 128]), op=ALU.mult),
                    r=BK(5) + [("psct",)], w=mt.pg(0, 512))

            def S5(i):
                mx = mix[i % 2]
                mt = mixT[i % 3]
                mtv = mt.ap.rearrange("p (k t) -> p k t", k=NK)
                pTv = banks[5][:].bitcast(BF16)[:, 0:512].rearrange("p (n t) -> p n t", n=4)
                for n in range(4):
                    P.add("pe", lambda e, n=n, pTv=pTv, mx=mx: e.transpose(out=pTv[:, n, :], in_=mx.ap[:, n * 128:(n + 1) * 128], identity=ident[:]),
                          r=mx.pg(n * 128, 128) + [("ident",)], w=BK(5))
                P.add("act", lambda e, mtv=mtv, pTv=pTv: e.activation(out=mtv[:, 4:8, :], in_=pTv, func=AF.Copy), r=BK(5), w=mt.pg(512, 512))

            def S6(i):
                c = order[i]
                s4 = i % 4
                mt = mixT[i % 3]
                mtv = mt.ap.rearrange("p (k t) -> p k t", k=NK)
                h = hl[s4]
                for half in range(2):
                    bk = 6 + half
                    for k in range(NK):
                        P.add("pe", lambda e, bk=bk, k=k, half=half, mtv=mtv: e.matmul(
                            banks[bk][:, :], lhsT=mtv[:, k, :], rhs=WOv[:, k, half * 512:(half + 1) * 512], start=(k == 0), stop=(k == NK - 1)),
                            r=mt.pg(k * 128, 128) + WO.pg(k * D + half * 512, 512), w=BK(bk))
                    hh = h.ap[:, half * 512:(half + 1) * 512]
                    P.add("dve", lambda e, bk=bk, hh=hh: e.tensor_tensor(out=hh, in0=banks[bk][:, :], in1=hh, op=ALU.add),
                          r=BK(bk) + h.pg(half * 512, 512), w=h.pg(half * 512, 512))
                P.add("sp", lambda e, h=h, c=c: e.dma_start(out=H_d[c * 128:(c + 1) * 128, :], in_=h.ap),
                      r=h.pg(), w=[("dram", "H", c)], dma=("hl", s4))

            for i in range(min(3, NCH)):
                ld_rec(i)
            if fwd:
                ld_b(0)
            for k in range(-1, NCH + 4):
                if k >= 0 and ok(k + 3):
                    ld_rec(k + 3)
                if fwd and k >= 0 and ok(k + 1):
                    ld_b(k + 1)
                if fwd and ok(k - 2):
                    ld_h(k - 2)
                if ok(k + 1):
                    A1_pe(k + 1)
                if ok(k):
                    A2_s1(k)
                if fwd and ok(k - 1):
                    B1(k - 1)
                if ok(k):
                    A2_upd(k)
                if fwd and ok(k - 2):
                    B2(k - 2)
                if ok(k + 1):
                    A1_dve(k + 1)
                if fwd and ok(k - 4):
                    S6(k - 4)
                if fwd and ok(k - 1):
                    PP1(k - 1)
                if fwd and ok(k - 2):
                    PP2(k - 2)
                if fwd and ok(k - 3):
                    S5(k - 3)
                if ok(k):
                    A2_pso(k)

        if 3 in phases:
            run_dir(1)
        if 4 in phases:
            run_dir(0)

    if 1 in phases:
        load_ffn_w(0, "gd")
        ffn_phase("p1", x_d, "x", H_d, "H", 0, None, 0)
    if 2 in phases:
        phase2()
    if 3 in phases or 4 in phases:
        if 5 in phases:
            load_ffn_w(1, "g")
        phase3()
    if 5 in phases:
        if not (3 in phases or 4 in phases):
            load_ffn_w(1, "g")
        load_ffn_w(1, "d")
        ffn_phase("p4", H_d, "H", out_d, "out", 2, 3, 24)

    P.add("sp", None, r=[k for k in P.lw.keys() if k[0] == "dram"])
    P.emit(nc, es)
    es.close()
    return nc


def _prep_shared(inp):
    ident, maskf, maskb, scanm, Bl, rcl = _consts()
    f = lambda a: np.ascontiguousarray(np.asarray(a, dtype=np.float32))
    hl = f(inp["hgrn_lb"]).reshape(2, 2, 4, 128).transpose(3, 0, 1, 2).reshape(128, 16)
    psc = f(inp["pool_scale"]).reshape(4, 128).T
    nw = np.stack([f(inp["norm_ffn1"])[0], f(inp["norm_mix"])[0], f(inp["norm_ffn2"])[0], f(inp["norm_final"])], 0)
    return {
        "wg1": f(inp["w_ffn1_gate"])[0], "wu1": f(inp["w_ffn1_up"])[0], "wd1": f(inp["w_ffn1_down"])[0],
        "wg2": f(inp["w_ffn2_gate"])[0], "wu2": f(inp["w_ffn2_up"])[0], "wd2": f(inp["w_ffn2_down"])[0],
        "win": f(inp["w_in"])[0], "wout": f(inp["w_out"])[0], "poolw": f(inp["pool_w"])[0],
        "nw": f(nw), "lbp": f(hl), "psc": f(psc), "gn": f(inp["hgrn_gnorm"]).reshape(1, 128),
        "c_ident": ident, "c_maskf": maskf, "c_maskb": maskb, "c_scanm": scanm, "c_bmat": Bl, "c_rc": rcl,
    }


_NC_CACHE = {}


def kernel(**inputs):
    x = np.asarray(inputs["x"], dtype=np.float32)
    B, S, _ = x.shape
    shared = _prep_shared(inputs)
    if S not in _NC_CACHE:
        _NC_CACHE[S] = build(S)
    nc = _NC_CACHE[S]
    in_maps = []
    for b in range(B):
        m = dict(shared)
        m["x"] = np.ascontiguousarray(x[b])
        in_maps.append(m)
    res = run_bass_kernel_spmd(nc, in_maps, core_ids=list(range(B)))
    return np.stack([np.asarray(r["out"], dtype=np.float32) for r in res.results], 0)
```

```python
import numpy as np
from contextlib import ExitStack
import concourse.bass as bass
import concourse.mybir as mybir
from concourse.bass_utils import run_bass_kernel_spmd

F32 = mybir.dt.float32
BF16 = mybir.dt.bfloat16
AF = mybir.ActivationFunctionType
ALU = mybir.AluOpType

D = 1024
DFF = 2816
NF = DFF // 128
NK = D // 128
SEQ = 8192
NCORES = 8
EPS = 1e-6
POOL_WINDOWS = (2, 4, 8, 16)


class _Op:
    __slots__ = ("eng", "fn", "deps", "dma", "needs_inc", "semkey", "val", "waits")


class Prog:
    COMPUTE = ("pe", "act", "dve", "pool")

    def __init__(self):
        self.ops = []
        self.lw = {}
        self.rd = {}

    def add(self, eng, fn, r=(), w=(), dma=None):
        op = _Op()
        op.eng, op.fn, op.dma = eng, fn, dma
        op.needs_inc = False
        op.semkey = None
        op.val = 0
        deps = []
        for k in r:
            x = self.lw.get(k)
            if x is not None:
                deps.append(x)
        for k in w:
            x = self.lw.get(k)
            if x is not None:
                deps.append(x)
            rr = self.rd.get(k)
            if rr:
                for v in rr.values():
                    if isinstance(v, list):
                        deps.extend(v)
                    else:
                        deps.append(v)
        for k in r:
            rr = self.rd.setdefault(k, {})
            if dma is not None:
                rr.setdefault("_dma", []).append(op)
            else:
                rr[eng] = op
        for k in w:
            self.lw[k] = op
            self.rd[k] = {}
        dd = []
        seen = set()
        for d_ in deps:
            if d_ is op or id(d_) in seen:
                continue
            seen.add(id(d_))
            if d_.eng == "pe" and eng == "pe" and d_.dma is None and dma is None:
                continue
            d_.needs_inc = True
            dd.append(d_)
        op.deps = dd
        self.ops.append(op)
        return op

    def finalize(self):
        cnt = {}
        waited = {}
        for op in self.ops:
            key = ("dma", op.dma) if op.dma is not None else ("eng", op.eng)
            need = {}
            for d_ in op.deps:
                if d_.val > need.get(d_.semkey, 0):
                    need[d_.semkey] = d_.val
            ws = []
            wd = waited.setdefault(op.eng, {})
            for sk, v in need.items():
                if wd.get(sk, 0) >= v:
                    continue
                wd[sk] = v
                ws.append((sk, v))
            op.waits = ws
            if op.needs_inc:
                cnt[key] = cnt.get(key, 0) + (16 if op.dma is not None else 1)
                op.semkey = key
                op.val = cnt[key]
        return list(cnt.keys())

    def emit(self, nc, es):
        keys = self.finalize()
        sems = {}
        for i, k in enumerate(keys):
            sems[k] = es.enter_context(nc.semaphore("s%d" % i))
        block = es.enter_context(nc.Block())
        per = {}
        for op in self.ops:
            per.setdefault(op.eng, []).append(op)

        def run(eng_name):
            def body(e):
                for op in per.get(eng_name, []):
                    for sk, v in op.waits:
                        e.wait_ge(sems[sk], v)
                    if op.fn is None:
                        continue
                    ins = op.fn(e)
                    if op.needs_inc:
                        ins.then_inc(sems[op.semkey], 16 if op.dma is not None else 1)
            return body

        block.sync(run("sp"))
        block.scalar(run("act"))
        block.vector(run("dve"))
        block.gpsimd(run("pool"))
        block.tensor(run("pe"))


def _pool_tables():
    Sv = 384
    B = np.zeros((3, 3, 4, 128, 128), np.float32)
    rc = np.zeros((3, 4, 128), np.float32)
    for g, w in enumerate(POOL_WINDOWS):
        full = np.zeros((Sv, Sv), np.float32)
        cnts = np.zeros(Sv, np.float32)
        for t in range(Sv):
            lo = min(max(t - w // 2, 0), Sv)
            hi = min(max(t + w - w // 2, 0), Sv)
            full[lo:hi, t] = 1.0
            cnts[t] = hi - lo
            full[t, t] -= cnts[t]
        for var in range(3):
            c = var
            rc[var, g] = 1.0 / cnts[c * 128:(c + 1) * 128]
            for pos in range(3):
                sc = c + pos - 1
                if 0 <= sc < 3:
                    B[var, pos, g] = full[sc * 128:(sc + 1) * 128, c * 128:(c + 1) * 128]
    return B, rc


def _consts():
    ident = np.eye(128, dtype=np.float32)
    j = np.arange(128)[:, None]
    t = np.arange(128)[None, :]
    maskf = (j <= t).astype(np.float32)
    maskb = (j >= t).astype(np.float32)
    scanm = np.ones((128, 512), np.float32)
    scanm[:, 0::128] = 0.0
    B, rc = _pool_tables()
    Bl = np.ascontiguousarray(B.reshape(36, 128, 128).transpose(1, 0, 2)).reshape(128, 36 * 128)
    rcl = np.ascontiguousarray(rc.reshape(1, 12 * 128))
    return ident, maskf, maskb, scanm, Bl, rcl


PAGE = 256
ARENA_BYTES = 204 * 1024
OFF_WA, OFF_WB, OFF_WC, OFF_WORK = 0, 44 * 1024, 88 * 1024, 132 * 1024


class T_:
    def __init__(self, arena, off, nelem, dt):
        self.off, self.nelem, self.dt = off, nelem, dt
        self.esz = 4 if dt == F32 else 2
        assert off % 4 == 0
        nb = (nelem * self.esz + 3) // 4 * 4
        assert off + nb <= ARENA_BYTES, ("arena overflow", off, nb)
        a = arena[:, off // 4:(off + nb) // 4]
        if dt != F32:
            a = a.bitcast(BF16)
        self.ap = a[:, 0:nelem]

    def pg(self, e0=0, n=None):
        if n is None:
            n = self.nelem - e0
        b0 = self.off + e0 * self.esz
        b1 = self.off + (e0 + n) * self.esz - 1
        return [("pg", i) for i in range(b0 // PAGE, b1 // PAGE + 1)]


class Carver:
    def __init__(self, arena, start, end=ARENA_BYTES):
        self.arena, self.off, self.end = arena, start, end

    def tile(self, nelem, dt):
        t = T_(self.arena, self.off, nelem, dt)
        nb = nelem * t.esz
        self.off += (nb + PAGE - 1) // PAGE * PAGE
        assert self.off <= self.end, ("carver overflow", self.off, self.end)
        return t


def build(S, debug=False, phases=(1, 2, 3, 4, 5), ffn_nt=2):
    assert S % 512 == 0
    NCH = S // 128
    NB2 = S // 512
    nc = bass.Bass("TRN2", target_bir_lowering=False)

    def din(name, shape):
        return nc.dram_tensor(name, shape, F32, kind="ExternalInput").ap()

    skind = "ExternalOutput" if debug else "Internal"

    def dscr(name, shape, dt):
        return nc.dram_tensor(name, shape, dt, kind=skind).ap()

    x_d = din("x", [S, D])
    wg_d = [din("wg1", [D, DFF]), din("wg2", [D, DFF])]
    wu_d = [din("wu1", [D, DFF]), din("wu2", [D, DFF])]
    wd_d = [din("wd1", [DFF, D]), din("wd2", [DFF, D])]
    win_d = din("win", [D, 3072])
    wout_d = din("wout", [D, D])
    poolw_d = din("poolw", [4, 128, 128])
    nw_d = din("nw", [4, D])
    lbp_d = din("lbp", [128, 16])
    psc_d = din("psc", [128, 4])
    gn_d = din("gn", [1, 128])
    ident_d = din("c_ident", [128, 128])
    maskf_d = din("c_maskf", [128, 128])
    maskb_d = din("c_maskb", [128, 128])
    scanm_d = din("c_scanm", [128, 512])
    bmat_d = din("c_bmat", [128, 36 * 128])
    rc_d = din("c_rc", [1, 12 * 128])
    out_d = nc.dram_tensor("out", [S, D], F32, kind="ExternalOutput").ap()

    H_d = dscr("H", [S, D], F32)
    REC_d = [dscr("REC0", [NCH, 128, 1536], BF16), dscr("REC1", [NCH, 128, 1536], BF16)]
    VV_d = dscr("VV", [NCH, 128, 512], BF16)
    G_d = dscr("G", [NCH, 128, 512], F32)
    Z_d = dscr("Z", [NCH, 128, 512], BF16)
    OB_d = dscr("OB", [NCH, 128, 512], F32)

    P = Prog()
    es = ExitStack()

    def sb(name, shape, dt):
        return es.enter_context(nc.sbuf_tensor(name, shape, dt))

    arena = sb("arena", [128, ARENA_BYTES // 4], F32)
    ident = sb("ident", [128, 128], BF16)
    expT = sb("expT", [128, 2 * 4 * NCH], F32)
    expTv = expT[:].rearrange("p (d n c) -> p d n c", d=2, n=4)
    EPS_T = sb("eps_t", [128, 8], F32)
    EPS_AP = EPS_T[:]
    stt = sb("stt", [128, 64], F32)
    lbt = sb("lbt", [128, 16], F32)
    lbs = sb("lbs", [128, 32], F32)
    psct = sb("psct", [128, 4], F32)
    banks = [es.enter_context(nc.psum_tensor("bank%d" % i, [128, 512], F32)) for i in range(8)]

    WA = T_(arena, OFF_WA, NK * DFF, BF16)
    WB = T_(arena, OFF_WB, NK * DFF, BF16)
    WC = T_(arena, OFF_WC, NF * D, BF16)
    WAv = WA.ap.rearrange("p (k f) -> p k f", k=NK)
    WBv = WB.ap.rearrange("p (k f) -> p k f", k=NK)
    WCv = WC.ap.rearrange("p (f d) -> p f d", f=NF)

    P.add("pool", lambda e: e.memset(EPS_AP[:, 0:1], EPS), w=[("epsap",)])
    P.add("pool", lambda e: e.memset(EPS_AP[:, 1:8], -0.5), w=[("epsap",)])
    P.add("pool", lambda e: e.dma_start(out=ident[:], in_=ident_d), w=[("ident",)], dma="ident")

    def BK(i):
        return [("bank", i)]

    def load_ffn_w(li, which):
        if "g" in which:
            for (T, dstv, src, nm) in ((WA, WAv, wg_d[li], "WA"), (WB, WBv, wu_d[li], "WB")):
                srcv = src.rearrange("(k p) f -> p k f", p=128)
                for pc in range(4):
                    c0, c1 = pc * 704, (pc + 1) * 704
                    wl = []
                    for k in range(NK):
                        wl += T.pg(k * DFF + c0, 704)
                    P.add("pool", lambda e, dstv=dstv, srcv=srcv, c0=c0, c1=c1: e.dma_start(out=dstv[:, :, c0:c1], in_=srcv[:, :, c0:c1]),
                          w=wl, dma=(nm, pc))
        if "d" in which:
            srcv = wd_d[li].rearrange("(f p) d -> p f d", p=128)
            for pc in range(2):
                f0, f1 = pc * 11, (pc + 1) * 11
                P.add("pool", lambda e, srcv=srcv, f0=f0, f1=f1: e.dma_start(out=WCv[:, f0:f1, :], in_=srcv[:, f0:f1, :]),
                      w=WC.pg(f0 * D, 11 * D), dma=("WC", pc))

    def rms_ops(tagp, col, x_ap, x_pg, junk_ap, junk_pg, out_ap, out_pg, nwrow, n=D):
        sq = stt[:, col:col + 1]
        rs = stt[:, 32 + col:33 + col]
        k1, k2 = ("st", col), ("st", 32 + col)
        P.add("act", lambda e: e.activation(out=junk_ap, in_=x_ap, func=AF.Square, accum_out=sq), r=x_pg, w=junk_pg + [k1])
        P.add("pool", lambda e: e.tensor_scalar(out=rs, in0=sq, scalar1=1.0 / n, scalar2=EPS, op0=ALU.mult, op1=ALU.add),
              r=[k1], w=[k2])
        P.add("pool", lambda e: e.tensor_tensor(out=rs, in0=rs, in1=EPS_AP[:, 1:2], op=ALU.pow), r=[k2, ("epsap",)], w=[k2])
        P.add("dve", lambda e: e.scalar_tensor_tensor(out=out_ap, in0=x_ap, scalar=rs, in1=nwrow, op0=ALU.mult, op1=ALU.mult),
              r=x_pg + [k2, ("nwt",)], w=out_pg)

    def ffn_phase(tagp, src_d, src_nm, dst_d, dst_nm, nw_i, final_i, stcol):
        cv = Carver(arena, OFF_WORK)
        nwt = cv.tile(2 * D, F32)
        nwv = nwt.ap.rearrange("p (a d) -> p a d", a=2)
        P.add("sp", lambda e: e.dma_start(out=nwv[:, 0, :], in_=nw_d[nw_i:nw_i + 1, :].rearrange("a d -> (a d)").partition_broadcast(128)),
              w=nwt.pg(0, D) + [("nwt",)], dma=("nwt", tagp, 0))
        if final_i is not None:
            P.add("sp", lambda e: e.dma_start(out=nwv[:, 1, :], in_=nw_d[final_i:final_i + 1, :].rearrange("a d -> (a d)").partition_broadcast(128)),
                  w=nwt.pg(D, D) + [("nwt",)], dma=("nwt", tagp, 1))
        NX = 3 * ffn_nt
        xt = [cv.tile(D, F32) for _ in range(NX)]
        xn = [cv.tile(D, BF16) for _ in range(2)]
        xnT = [cv.tile(NK * 128 * ffn_nt, BF16) for _ in range(2)]
        hT = cv.tile(NF * 128 * ffn_nt, BF16)
        sg = [cv.tile(128 * ffn_nt, BF16) for _ in range(2)]
        TW = 128 * ffn_nt
        blocks = []
        t0 = 0
        ntiles = S // 128
        while t0 < ntiles:
            n = min(ffn_nt, ntiles - t0)
            blocks.append((t0, n))
            t0 += n
        gtile = [0]

        def do_load(b):
            tb, n = blocks[b]
            for i in range(n):
                ti = tb + i
                xa = xt[ti % NX]
                P.add("sp", lambda e, xa=xa, ti=ti: e.dma_start(out=xa.ap, in_=src_d[ti * 128:(ti + 1) * 128, :]),
                      r=[("dram", src_nm, ti)], w=xa.pg(), dma=("xt", tagp, ti % NX))

        def do_norm(b):
            tb, n = blocks[b]
            for i in range(n):
                ti = tb + i
                xa = xt[ti % NX]
                xs = xn[ti % 2]
                rms_ops(tagp, stcol + ti % 4, xa.ap, xa.pg(), xs.ap, xs.pg(), xs.ap, xs.pg(), nwv[:, 0, :])

        def do_transpose(b):
            tb, n = blocks[b]
            xT = xnT[b % 2]
            xTv = xT.ap.rearrange("p (k t) -> p k t", k=NK)
            for i in range(n):
                ti = tb + i
                xs = xn[ti % 2]
                bk = 6 + (ti % 2)
                pv = banks[bk][:].bitcast(BF16).rearrange("p (k t) -> p k t", k=NK)
                for k in range(NK):
                    P.add("pe", lambda e, pv=pv, xs=xs, k=k: e.transpose(out=pv[:, k, :], in_=xs.ap[:, k * 128:(k + 1) * 128], identity=ident[:]),
                          r=xs.pg(k * 128, 128) + [("ident",)], w=BK(bk))
                wl = []
                for k in range(NK):
                    wl += xT.pg(k * TW + i * 128, 128)
                P.add("act", lambda e, pv=pv, xTv=xTv, i=i: e.activation(out=xTv[:, :, i * 128:(i + 1) * 128], in_=pv, func=AF.Copy),
                      r=BK(bk), w=wl)

        def do_gateup(b, mid_hook):
            tb, n = blocks[b]
            ntok = n * 128
            xT = xnT[b % 2]
            xTv = xT.ap.rearrange("p (k t) -> p k t", k=NK)
            hTv = hT.ap.rearrange("p (f t) -> p f t", f=NF)
            for f in range(NF):
                if f == NF // 2 and mid_hook is not None:
                    mid_hook()
                pg_, pu_ = banks[f % 2], banks[2 + f % 2]
                for (W, Wv, pp, bki) in ((WA, WAv, pg_, f % 2), (WB, WBv, pu_, 2 + f % 2)):
                    for k in range(NK):
                        P.add("pe", lambda e, pp=pp, Wv=Wv, k=k, f=f, xTv=xTv, ntok=ntok: e.matmul(
                            pp[:, 0:ntok], lhsT=Wv[:, k, f * 128:(f + 1) * 128], rhs=xTv[:, k, 0:ntok], start=(k == 0), stop=(k == NK - 1)),
                            r=xT.pg(k * TW, ntok) + W.pg(k * DFF + f * 128, 128), w=BK(bki))
                sgt = sg[f % 2]
                P.add("act", lambda e, sgt=sgt, pg_=pg_, ntok=ntok: e.activation(out=sgt.ap[:, 0:ntok], in_=pg_[:, 0:ntok], func=AF.Silu),
                      r=BK(f % 2), w=sgt.pg())
                P.add("dve", lambda e, sgt=sgt, pu_=pu_, f=f, hTv=hTv, ntok=ntok: e.tensor_tensor(
                    out=hTv[:, f, 0:ntok], in0=pu_[:, 0:ntok], in1=sgt.ap[:, 0:ntok], op=ALU.mult),
                    r=sgt.pg() + BK(2 + f % 2), w=hT.pg(f * TW, ntok))

        def do_down(b):
            tb, n = blocks[b]
            hTv = hT.ap.rearrange("p (f t) -> p f t", f=NF)
            for i in range(n):
                ti = tb + i
                xa = xt[ti % NX]
                for half in range(2):
                    j = gtile[0]
                    gtile[0] += 1
                    bk = 4 + j % 2
                    pd_ = banks[bk]
                    for f in range(NF):
                        P.add("pe", lambda e, pd_=pd_, f=f, i=i, half=half: e.matmul(
                            pd_[:, :], lhsT=hTv[:, f, i * 128:(i + 1) * 128], rhs=WCv[:, f, half * 512:(half + 1) * 512],
                            start=(f == 0), stop=(f == NF - 1)),
                            r=hT.pg(f * TW + i * 128, 128) + WC.pg(f * D + half * 512, 512), w=BK(bk))
                    xh = xa.ap[:, half * 512:(half + 1) * 512]
                    P.add("dve", lambda e, pd_=pd_, xh=xh: e.scalar_tensor_tensor(
                        out=xh, in0=pd_[:, :], scalar=0.5, in1=xh, op0=ALU.mult, op1=ALU.add),
                        r=BK(bk) + xa.pg(half * 512, 512), w=xa.pg(half * 512, 512))
                if final_i is not None:
                    xs = xn[ti % 2]
                    rms_ops(tagp, stcol + 4 + ti % 4, xa.ap, xa.pg(), xs.ap, xs.pg(), xa.ap, xa.pg(), nwv[:, 1, :])
                P.add("sp", lambda e, xa=xa, ti=ti: e.dma_start(out=dst_d[ti * 128:(ti + 1) * 128, :], in_=xa.ap),
                      r=xa.pg(), w=[("dram", dst_nm, ti)], dma=("xt", tagp, ti % NX))

        nb = len(blocks)
        do_load(0)
        if nb > 1:
            do_load(1)
        do_norm(0)
        do_transpose(0)
        for b in range(nb):
            if b + 2 < nb:
                do_load(b + 2)
            hook = (lambda b=b: do_norm(b + 1)) if b + 1 < nb else None
            do_gateup(b, hook)
            if b + 1 < nb:
                do_transpose(b + 1)
            do_down(b)

    def phase2():
        cv = Carver(arena, 0)
        WIN = cv.tile(NK * 3072, BF16)
        WINv = WIN.ap.rearrange("p (k c) -> p k c", k=NK)
        srcv = win_d.rearrange("(k p) c -> p k c", p=128)
        for pc in range(6):
            wl = []
            for k in range(NK):
                wl += WIN.pg(k * 3072 + pc * 512, 512)
            P.add("pool", lambda e, pc=pc: e.dma_start(out=WINv[:, :, pc * 512:(pc + 1) * 512], in_=srcv[:, :, pc * 512:(pc + 1) * 512]),
                  w=wl, dma=("WIN", pc))
        nwt = cv.tile(D, F32)
        P.add("sp", lambda e: e.dma_start(out=nwt.ap, in_=nw_d[1:2, :].rearrange("a d -> (a d)").partition_broadcast(128)),
              w=nwt.pg() + [("nwt",)], dma=("nwt", "p2", 0))
        gnb = cv.tile(512, F32)
        gnbv = gnb.ap.rearrange("p (n v) -> p n v", n=4)
        for n in range(4):
            P.add("sp", lambda e, n=n: e.dma_start(out=gnbv[:, n, :], in_=gn_d.rearrange("a d -> (a d)").partition_broadcast(128)),
                  w=gnb.pg(n * 128, 128), dma=("gnb", n))
        scm = cv.tile(512, F32)
        P.add("sp", lambda e: e.dma_start(out=scm.ap, in_=scanm_d), w=scm.pg(), dma=("scm",))
        P.add("sp", lambda e: e.dma_start(out=lbt[:], in_=lbp_d), w=[("lbt",)], dma=("lbt",))
        lbtv = lbt[:].rearrange("p (d l n) -> p d l n", d=2, l=2)
        lbv = lbs[:, 0:8].rearrange("p (d n) -> p d n", d=2)
        P.add("dve", lambda e: e.tensor_tensor(out=lbv, in0=lbtv[:, :, 0, :], in1=lbtv[:, :, 1, :], op=ALU.subtract), r=[("lbt",)], w=[("lbs", 0)])
        P.add("act", lambda e: e.activation(out=lbs[:, 0:8], in_=lbs[:, 0:8], func=AF.Sigmoid), r=[("lbs", 0)], w=[("lbs", 0)])
        P.add("dve", lambda e: e.tensor_scalar(out=lbs[:, 8:16], in0=lbs[:, 0:8], scalar1=-1.0, scalar2=1.0, op0=ALU.mult, op1=ALU.add),
              r=[("lbs", 0)], w=[("lbs", 1)])
        P.add("dve", lambda e: e.tensor_scalar(out=lbs[:, 16:24], in0=lbs[:, 0:8], scalar1=1.0, scalar2=-1.0, op0=ALU.mult, op1=ALU.add),
              r=[("lbs", 0)], w=[("lbs", 2)])
        LBK = [("lbs", 0), ("lbs", 1), ("lbs", 2)]

        ht = [cv.tile(D, F32) for _ in range(4)]
        xn = [cv.tile(D, BF16) for _ in range(2)]
        uT = [cv.tile(NK * 512, BF16) for _ in range(1)]
        sig = [[cv.tile(512, F32) for _ in range(8)] for _ in range(2)]
        tq = [[cv.tile(512, F32) for _ in range(3)] for _ in range(4)]
        qs = [[cv.tile(512, F32) for _ in range(4)] for _ in range(2)]
        RST = [cv.tile(4 * 1536, BF16) for _ in range(2)]
        RSTv = [r.ap.rearrange("p (c k n t) -> p c k n t", c=4, k=3, n=4) for r in RST]
        GST = cv.tile(4 * 512, F32)
        ZST = cv.tile(4 * 512, BF16)
        VST = cv.tile(4 * 512, BF16)
        gtmp = [cv.tile(512, F32) for _ in range(2)]
        junk2 = cv.tile(D, BF16)
        NHT = 4

        def rst_pg(d, c, kk_, n):
            return RST[d].pg(((c * 3 + kk_) * 4 + n) * 128, 128)

        def load_h(b):
            for i in range(4):
                ti = b * 4 + i
                xa = ht[ti % NHT]
                P.add("sp", lambda e, xa=xa, ti=ti: e.dma_start(out=xa.ap, in_=H_d[ti * 128:(ti + 1) * 128, :]),
                      r=[("dram", "H", ti)], w=xa.pg(), dma=("ht", ti % NHT))

        cnt = [0]

        def nb_():
            cnt[0] += 1
            return cnt[0]

        u = uT[0]
        uv = u.ap.rearrange("p (k t) -> p k t", k=NK)

        def norm_stage(b):
            tiles = [b * 4 + i for i in range(4)]
            for ti in tiles:
                xa, xs, col = ht[ti % NHT], xn[ti % 2], 8 + ti % 4
                sq = stt[:, col:col + 1]
                P.add("act", lambda e, xa=xa, sq=sq: e.activation(out=junk2.ap, in_=xa.ap, func=AF.Square, accum_out=sq),
                      r=xa.pg(), w=junk2.pg() + [("st", col)])
            for ti in tiles:
                col = 8 + ti % 4
                sq, rs = stt[:, col:col + 1], stt[:, 32 + col:33 + col]
                P.add("pool", lambda e, sq=sq, rs=rs: e.tensor_scalar(out=rs, in0=sq, scalar1=1.0 / D, scalar2=EPS, op0=ALU.mult, op1=ALU.add),
                      r=[("st", col)], w=[("st", 32 + col)])
                P.add("pool", lambda e, rs=rs: e.tensor_tensor(out=rs, in0=rs, in1=EPS_AP[:, 1:2], op=ALU.pow),
                      r=[("st", 32 + col), ("epsap",)], w=[("st", 32 + col)])

        def xn_op(ti):
            xa, xs, col = ht[ti % NHT], xn[ti % 2], 8 + ti % 4
            rs = stt[:, 32 + col:33 + col]
            P.add("dve", lambda e, xa=xa, xs=xs, rs=rs: e.scalar_tensor_tensor(out=xs.ap, in0=xa.ap, scalar=rs, in1=nwt.ap, op0=ALU.mult, op1=ALU.mult),
                  r=xa.pg() + [("st", 32 + col), ("nwt",)], w=xs.pg())

        def unit_T(b, i):
            ti = b * 4 + i
            xs = xn[ti % 2]
            bk = 6 + (ti % 2)
            pv = banks[bk][:].bitcast(BF16).rearrange("p (k t) -> p k t", k=NK)
            for k in range(NK):
                P.add("pe", lambda e, pv=pv, xs=xs, k=k: e.transpose(out=pv[:, k, :], in_=xs.ap[:, k * 128:(k + 1) * 128], identity=ident[:]),
                      r=xs.pg(k * 128, 128) + [("ident",)], w=BK(bk))
            wl = []
            for k in range(NK):
                wl += u.pg(k * 512 + i * 128, 128)
            P.add("act", lambda e, pv=pv, i=i: e.activation(out=uv[:, :, i * 128:(i + 1) * 128], in_=pv, func=AF.Copy),
                  r=BK(bk), w=wl)
            if i + 2 < 4:
                xn_op(ti + 2)

        def unit_tok(b, c0, kind, i):
            bk = nb_() % 2
            pp = banks[bk]
            for k in range(NK):
                P.add("pe", lambda e, pp=pp, k=k, i=i, c0=c0: e.matmul(
                    pp[:, :], lhsT=uv[:, k, i * 128:(i + 1) * 128], rhs=WINv[:, k, c0:c0 + 512], start=(k == 0), stop=(k == NK - 1)),
                    r=u.pg(k * 512 + i * 128, 128) + WIN.pg(k * 3072 + c0, 512), w=BK(bk))
            if kind == "z":
                P.add("act", lambda e, pp=pp, i=i: e.activation(out=ZST.ap[:, i * 512:(i + 1) * 512], in_=pp[:, :], func=AF.Copy),
                      r=BK(bk), w=ZST.pg(i * 512, 512))
            elif kind == "v":
                P.add("act", lambda e, pp=pp, i=i: e.activation(out=VST.ap[:, i * 512:(i + 1) * 512], in_=pp[:, :], func=AF.Copy),
                      r=BK(bk), w=VST.pg(i * 512, 512))
            else:
                gt = gtmp[i % 2]
                P.add("act", lambda e, pp=pp, gt=gt: e.activation(out=gt.ap, in_=pp[:, :], func=AF.Silu), r=BK(bk), w=gt.pg())
                P.add("pool", lambda e, gt=gt, i=i: e.tensor_tensor(out=GST.ap[:, i * 512:(i + 1) * 512], in0=gt.ap, in1=gnb.ap, op=ALU.mult),
                      r=gt.pg() + gnb.pg(), w=GST.pg(i * 512, 512))

        def stores_A1(b):
            P.add("sp", lambda e, b=b: e.dma_start(out=VV_d[b * 4:(b + 1) * 4].rearrange("c p f -> p c f"), in_=VST.ap.rearrange("p (c f) -> p c f", c=4)),
                  r=VST.pg(), w=[("dram", "VV", b)], dma=("VST",))
            P.add("sp", lambda e, b=b: e.dma_start(out=Z_d[b * 4:(b + 1) * 4].rearrange("c p f -> p c f"), in_=ZST.ap.rearrange("p (c f) -> p c f", c=4)),
                  r=ZST.pg(), w=[("dram", "Z", b)], dma=("ZST",))
            P.add("sp", lambda e, b=b: e.dma_start(out=G_d[b * 4:(b + 1) * 4].rearrange("c p f -> p c f"), in_=GST.ap.rearrange("p (c f) -> p c f", c=4)),
                  r=GST.pg(), w=[("dram", "G", b)], dma=("GST",))

        def A2(b):
            for n in range(4):
                bk = 2 + nb_() % 3
                pp = banks[bk]
                c0 = 512 + n * 128
                qt = qs[b % 2][n]
                for k in range(NK):
                    P.add("pe", lambda e, pp=pp, k=k, c0=c0: e.matmul(
                        pp[:, :], lhsT=WINv[:, k, c0:c0 + 128], rhs=uv[:, k, :], start=(k == 0), stop=(k == NK - 1)),
                        r=u.pg(k * 512, 512) + WIN.pg(k * 3072 + c0, 128), w=BK(bk))
                P.add("act", lambda e, pp=pp, qt=qt: e.activation(out=qt.ap, in_=pp[:, :], func=AF.Silu), r=BK(bk), w=qt.pg())
            for hd in range(8):
                bk = 2 + nb_() % 3
                pp = banks[bk]
                c0 = 1536 + hd * 128
                t0_ = sig[b % 2][hd]
                for k in range(NK):
                    P.add("pe", lambda e, pp=pp, k=k, c0=c0: e.matmul(
                        pp[:, :], lhsT=WINv[:, k, c0:c0 + 128], rhs=uv[:, k, :], start=(k == 0), stop=(k == NK - 1)),
                        r=u.pg(k * 512, 512) + WIN.pg(k * 3072 + c0, 128), w=BK(bk))
                P.add("act", lambda e, pp=pp, t0_=t0_: e.activation(out=t0_.ap, in_=pp[:, :], func=AF.Sigmoid), r=BK(bk), w=t0_.pg())

        def chain_steps(b, d):
            steps = []

            def s_fk():
                for n in range(4):
                    hd = d * 4 + n
                    t0_ = sig[b % 2][hd]
                    t1_, t2_, t3_ = tq[n]
                    P.add("dve", lambda e, t0_=t0_, t1_=t1_, hd=hd: e.tensor_scalar(
                        out=t1_.ap, in0=t0_.ap, scalar1=lbs[:, 8 + hd:9 + hd], scalar2=lbs[:, hd:hd + 1], op0=ALU.mult, op1=ALU.add),
                        r=t0_.pg() + LBK, w=t1_.pg())
                    P.add("dve", lambda e, t0_=t0_, t2_=t2_, hd=hd: e.tensor_scalar(
                        out=t2_.ap, in0=t0_.ap, scalar1=lbs[:, 16 + hd:17 + hd], scalar2=lbs[:, 8 + hd:9 + hd], op0=ALU.mult, op1=ALU.add),
                        r=t0_.pg() + LBK, w=t2_.pg())

            def s_ln(n):
                hd = d * 4 + n
                t0_ = sig[b % 2][hd]
                t1_, t2_, t3_ = tq[n]
                P.add("act", lambda e, t1_=t1_: e.activation(out=t1_.ap, in_=t1_.ap, func=AF.Ln), r=t1_.pg(), w=t1_.pg())
                P.add("dve", lambda e, t1_=t1_, t3_=t3_: e.tensor_tensor_scan(
                    out=t3_.ap, data0=scm.ap, data1=t1_.ap, initial=0.0, op0=ALU.mult, op1=ALU.add),
                    r=t1_.pg() + scm.pg(), w=t3_.pg())
                c3 = t3_.ap.rearrange("p (c t) -> p c t", c=4)
                a3 = t0_.ap.rearrange("p (c t) -> p c t", c=4)
                if d == 0:
                    P.add("dve", lambda e, c3=c3, a3=a3: e.tensor_tensor(
                        out=a3, in0=c3, in1=c3[:, :, 127:128].to_broadcast([128, 4, 128]), op=ALU.subtract),
                        r=t3_.pg(), w=t0_.pg())
                else:
                    P.add("dve", lambda e, t0_=t0_, t1_=t1_, t3_=t3_: e.tensor_tensor(out=t0_.ap, in0=t1_.ap, in1=t3_.ap, op=ALU.subtract),
                          r=t1_.pg() + t3_.pg(), w=t0_.pg())

            def s_exp(n):
                hd = d * 4 + n
                t0_ = sig[b % 2][hd]
                t1_, t2_, t3_ = tq[n]
                qt = qs[b % 2][n]
                c3 = t3_.ap.rearrange("p (c t) -> p c t", c=4)
                P.add("act", lambda e, c3=c3, n=n: e.activation(out=expTv[:, d, n, b * 4:(b + 1) * 4], in_=c3[:, :, 127], func=AF.Exp),
                      r=t3_.pg(), w=[("expT", d, n, b)])
                P.add("act", lambda e, t0_=t0_, t1_=t1_: e.activation(out=t1_.ap, in_=t0_.ap, func=AF.Exp), r=t0_.pg(), w=t1_.pg())
                P.add("act", lambda e, t0_=t0_, t3_=t3_: e.activation(out=t3_.ap, in_=t0_.ap, func=AF.Exp, scale=-1.0), r=t0_.pg(), w=t3_.pg())
                wl0, wl1 = [], []
                for c in range(4):
                    wl0 += rst_pg(d, c, 0, n)
                    wl1 += rst_pg(d, c, 1, n)
                P.add("dve", lambda e, n=n, t1_=t1_, qt=qt: e.tensor_tensor(
                    out=RSTv[d][:, :, 0, n, :], in0=qt.ap.rearrange("p (c t) -> p c t", c=4), in1=t1_.ap.rearrange("p (c t) -> p c t", c=4), op=ALU.mult),
                    r=qt.pg() + t1_.pg(), w=wl0)
                P.add("dve", lambda e, n=n, t2_=t2_, t3_=t3_: e.tensor_tensor(
                    out=RSTv[d][:, :, 1, n, :], in0=t2_.ap.rearrange("p (c t) -> p c t", c=4), in1=t3_.ap.rearrange("p (c t) -> p c t", c=4), op=ALU.mult),
                    r=t2_.pg() + t3_.pg(), w=wl1)

            steps.append(s_fk)
            for n in range(4):
                steps.append(lambda n=n: s_ln(n))
            for n in range(4):
                steps.append(lambda n=n: s_exp(n))
            return steps

        def BT(b, d):
            for n in range(4):
                bk = 6 + n % 2
                pv = banks[bk][:].bitcast(BF16)[:, 0:512].rearrange("p (c t) -> p c t", c=4)
                for c in range(4):
                    P.add("pe", lambda e, pv=pv, n=n, c=c: e.transpose(out=pv[:, c, :], in_=RSTv[d][:, c, 1, n, :], identity=ident[:]),
                          r=rst_pg(d, c, 1, n) + [("ident",)], w=BK(bk))
                wl2 = []
                for c in range(4):
                    wl2 += rst_pg(d, c, 2, n)
                P.add("act", lambda e, pv=pv, n=n: e.activation(out=RSTv[d][:, :, 2, n, :], in_=pv, func=AF.Copy), r=BK(bk), w=wl2)
            P.add("sp", lambda e: e.dma_start(
                out=REC_d[d][b * 4:(b + 1) * 4].rearrange("c p f -> p c f"), in_=RST[d].ap.rearrange("p (c f) -> p c f", c=4)),
                r=RST[d].pg(), w=[("dram", "REC%d" % d, b)], dma=("RST", d))

        def iteration(bA, bB):
            units = []
            if bA is not None:
                norm_stage(bA)
                xn_op(bA * 4)
                xn_op(bA * 4 + 1)
                for i in range(4):
                    units.append(lambda i=i: unit_T(bA, i))
                for (c0, kind) in ((0, "z"), (1024, "v")):
                    for i in range(4):
                        units.append(lambda c0=c0, kind=kind, i=i: unit_tok(bA, c0, kind, i))
            csteps = []
            if bB is not None:
                csteps = chain_steps(bB, 0) + chain_steps(bB, 1)
            ui, ci = 0, 0
            while ui < len(units) or ci < len(csteps):
                if ui < len(units):
                    units[ui]()
                    ui += 1
                    if ui == 4 and bA is not None and bA + 1 < NB2:
                        load_h(bA + 1)
                take = 2 if (ui % 2 == 0) else 1
                if ui >= len(units):
                    take = len(csteps)
                for _ in range(take):
                    if ci < len(csteps):
                        csteps[ci]()
                        ci += 1
            if bA is not None:
                for i in range(4):
                    unit_tok(bA, 2560, "g", i)
                stores_A1(bA)
                A2(bA)
            if bB is not None:
                BT(bB, 0)
                BT(bB, 1)

        load_h(0)
        iteration(0, None)
        for b in range(NB2):
            iteration(b + 1 if b + 1 < NB2 else None, b)


    def phase3():
        cv = Carver(arena, OFF_WC)
        WO = cv.tile(NK * D, BF16)
        WOv = WO.ap.rearrange("p (k d) -> p k d", k=NK)
        P.add("pool", lambda e: e.dma_start(out=WOv, in_=wout_d.rearrange("(k p) d -> p k d", p=128)), w=WO.pg(), dma=("WO",))
        BM = cv.tile(36 * 128, BF16)
        BMv = BM.ap.rearrange("p (m t) -> p m t", m=36)
        P.add("pool", lambda e: e.dma_start(out=BM.ap, in_=bmat_d), w=BM.pg(), dma=("BM",))
        RCB = cv.tile(12 * 128, F32)
        RCv = RCB.ap.rearrange("p (v g t) -> p v g t", v=3, g=4)
        P.add("sp", lambda e: e.dma_start(out=RCB.ap, in_=rc_d.rearrange("a d -> (a d)").partition_broadcast(128)), w=RCB.pg(), dma=("RCB",))
        MK = [cv.tile(128, F32), cv.tile(128, F32)]
        P.add("sp", lambda e: e.dma_start(out=MK[0].ap, in_=maskf_d), w=MK[0].pg(), dma=("MK", 0))
        P.add("sp", lambda e: e.dma_start(out=MK[1].ap, in_=maskb_d), w=MK[1].pg(), dma=("MK", 1))
        PW = cv.tile(512, BF16)
        PWv = PW.ap.rearrange("p (g d) -> p g d", g=4)
        P.add("pool", lambda e: e.dma_start(out=PWv, in_=poolw_d.rearrange("g c d -> c g d")), w=PW.pg(), dma=("PW",))
        P.add("sp", lambda e: e.dma_start(out=psct[:], in_=psc_d), w=[("psct",)], dma=("psct",))
        St = cv.tile(512, F32)
        S1 = cv.tile(512, F32)
        S1b = cv.tile(512, BF16)
        Sv = St.ap.rearrange("p (n v) -> p n v", n=4)
        S1v = S1.ap.rearrange("p (n v) -> p n v", n=4)
        S1bv = S1b.ap.rearrange("p (n v) -> p n v", n=4)
        NR = 4
        rec = [cv.tile(1536, BF16) for _ in range(NR)]
        vv = [cv.tile(512, BF16) for _ in range(NR)]
        PT = [cv.tile(512, BF16) for _ in range(2)]
        obl = [cv.tile(512, F32) for _ in range(3)]
        obst = obl[0:2]
        gl = [cv.tile(512, F32) for _ in range(4)]
        zl = [cv.tile(3 * 512, BF16) for _ in range(3)]
        hl = [cv.tile(D, F32) for _ in range(4)]
        ot = [cv.tile(512, F32) for _ in range(2)]
        junk = cv.tile(512, BF16)
        mix = [cv.tile(512, BF16) for _ in range(2)]
        mixT = [cv.tile(NK * 128, BF16) for _ in range(3)]
        pooledT = [cv.tile(512, BF16) for _ in range(2)]

        def run_dir(d):
            order = list(range(NCH)) if d == 0 else list(range(NCH - 1, -1, -1))
            fwd = (d == 0)
            P.add("pool", lambda e: e.memset(St.ap, 0.0), w=St.pg())
            bA = 0
            bSs = (1, 2)
            bO = 3

            def ok(i):
                return 0 <= i < NCH

            def ld_rec(i):
                c = order[i]
                sl = i % NR
                P.add("sp", lambda e, c=c, sl=sl: e.dma_start(out=rec[sl].ap, in_=REC_d[d][c]),
                      r=[("dram", "REC%d" % d, c // 4)], w=rec[sl].pg(), dma=("rec", sl))
                P.add("sp", lambda e, c=c, sl=sl: e.dma_start(out=vv[sl].ap, in_=VV_d[c]),
                      r=[("dram", "VV", c // 4)], w=vv[sl].pg(), dma=("vv", sl))

            def ld_b(i):
                c = order[i]
                s3 = i % 3
                s4 = i % 4
                P.add("sp", lambda e, c=c, s3=s3: e.dma_start(out=obl[s3].ap, in_=OB_d[c]), r=[("dram", "OB", c)], w=obl[s3].pg(), dma=("obl", s3))
                P.add("sp", lambda e, c=c, s4=s4: e.dma_start(out=gl[s4].ap, in_=G_d[c]), r=[("dram", "G", c // 4)], w=gl[s4].pg(), dma=("gl", s4))
                lo, hi = max(c - 1, 0), min(c + 1, NCH - 1)
                zv = zl[s3].ap.rearrange("p (j f) -> p j f", j=3)
                P.add("sp", lambda e, lo=lo, hi=hi, c=c, zv=zv: e.dma_start(
                    out=zv[:, lo - c + 1:hi - c + 2, :], in_=Z_d[lo:hi + 1].rearrange("c p f -> p c f")),
                    r=[("dram", "Z", lo // 4), ("dram", "Z", hi // 4)], w=zl[s3].pg(), dma=("zl", s3))

            def ld_h(i):
                c = order[i]
                s4 = i % 4
                P.add("sp", lambda e, c=c, s4=s4: e.dma_start(out=hl[s4].ap, in_=H_d[c * 128:(c + 1) * 128, :]),
                      r=[("dram", "H", c)], w=hl[s4].pg(), dma=("hl", s4))

            def views(i):
                sl = i % NR
                rv = rec[sl].ap.rearrange("p (k n t) -> p k n t", k=3, n=4)
                vvv = vv[sl].ap.rearrange("p (n v) -> p n v", n=4)
                return sl, rv, vvv

            def A1_pe(i):
                sl, rv, vvv = views(i)
                pA = banks[bA][:].rearrange("p (n t) -> p n t", n=4)
                bS = bSs[i % 2]
                pS = banks[bS][:].rearrange("p (n t) -> p n t", n=4)
                for n in range(4):
                    P.add("pe", lambda e, pA=pA, rv=rv, n=n: e.matmul(pA[:, n, :], lhsT=rv[:, 1, n, :], rhs=rv[:, 0, n, :], start=True, stop=True),
                          r=rec[sl].pg(), w=BK(bA))
                for n in range(4):
                    P.add("pe", lambda e, pS=pS, rv=rv, vvv=vvv, n=n: e.matmul(pS[:, n, :], lhsT=rv[:, 2, n, :], rhs=vvv[:, n, :], start=True, stop=True),
                          r=rec[sl].pg() + vv[sl].pg(), w=BK(bS))

            def A1_dve(i):
                pA = banks[bA][:].rearrange("p (n t) -> p n t", n=4)
                pt = PT[i % 2]
                ptv = pt.ap.rearrange("p (n t) -> p n t", n=4)
                P.add("dve", lambda e, pA=pA, ptv=ptv: e.tensor_tensor(
                    out=ptv, in0=pA, in1=MK[d].ap.unsqueeze(1).to_broadcast([128, 4, 128]), op=ALU.mult),
                    r=BK(bA) + MK[d].pg(), w=pt.pg())

            def A2_s1(i):
                c = order[i]
                P.add("dve", lambda e, c=c: e.tensor_tensor(out=S1v, in0=Sv, in1=expTv[:, d, :, c:c + 1].to_broadcast([128, 4, 128]), op=ALU.mult),
                      r=St.pg() + [("expT", d, n, c // 4) for n in range(4)], w=S1.pg())
                P.add("act", lambda e: e.activation(out=S1b.ap, in_=S1.ap, func=AF.Copy), r=S1.pg(), w=S1b.pg())

            def A2_upd(i):
                bS = bSs[i % 2]
                pS = banks[bS][:].rearrange("p (n t) -> p n t", n=4)
                P.add("dve", lambda e, pS=pS: e.tensor_tensor(out=Sv, in0=S1v, in1=pS, op=ALU.add), r=S1.pg() + BK(bS), w=St.pg())

            def A2_pso(i):
                c = order[i]
                sl, rv, vvv = views(i)
                pO = banks[bO][:].rearrange("p (n t) -> p n t", n=4)
                pt = PT[i % 2]
                ptv = pt.ap.rearrange("p (n t) -> p n t", n=4)
                for n in range(4):
                    P.add("pe", lambda e, pO=pO, rv=rv, n=n: e.matmul(pO[:, n, :], lhsT=rv[:, 0, n, :], rhs=S1bv[:, n, :], start=True, stop=False),
                          r=rec[sl].pg() + S1b.pg(), w=BK(bO))
                    P.add("pe", lambda e, pO=pO, ptv=ptv, vvv=vvv, n=n: e.matmul(pO[:, n, :], lhsT=ptv[:, n, :], rhs=vvv[:, n, :], start=False, stop=True),
                          r=pt.pg() + vv[sl].pg(), w=BK(bO))
                if not fwd:
                    ob = obst[i % 2]
                    P.add("act", lambda e, ob=ob: e.activation(out=ob.ap, in_=banks[bO][:, :], func=AF.Copy), r=BK(bO), w=ob.pg())
                    P.add("sp", lambda e, ob=ob, c=c: e.dma_start(out=OB_d[c], in_=ob.ap), r=ob.pg(), w=[("dram", "OB", c)], dma=("obl", i % 2))

            def B1(i):
                s3 = i % 3
                o_ = ot[i % 2]
                sc = 16 + 4 * (i % 2)
                rc_ = 48 + 4 * (i % 2)
                P.add("dve", lambda e, s3=s3, o_=o_: e.tensor_tensor(out=o_.ap, in0=banks[bO][:, :], in1=obl[s3].ap, op=ALU.add),
                      r=BK(bO) + obl[s3].pg(), w=o_.pg())
                for n in range(4):
                    P.add("act", lambda e, n=n, o_=o_, sc=sc: e.activation(out=junk.ap[:, n * 128:(n + 1) * 128], in_=o_.ap[:, n * 128:(n + 1) * 128],
                                                                           func=AF.Square, accum_out=stt[:, sc + n:sc + n + 1]),
                          r=o_.pg(n * 128, 128), w=junk.pg(n * 128, 128) + [("st", sc + n)])

            def B2(i):
                o_ = ot[i % 2]
                rc_ = 48 + 4 * (i % 2)
                s4 = i % 4
                otv = o_.ap.rearrange("p (n v) -> p n v", n=4)
                mx = mix[i % 2]
                sc = 16 + 4 * (i % 2)
                P.add("pool", lambda e, sc=sc, rc_=rc_: e.tensor_scalar(out=stt[:, rc_:rc_ + 4], in0=stt[:, sc:sc + 4], scalar1=1.0 / 128, scalar2=EPS,
                                                                       op0=ALU.mult, op1=ALU.add),
                      r=[("st", sc + n) for n in range(4)], w=[("st", rc_)])
                P.add("pool", lambda e, rc_=rc_: e.tensor_tensor(out=stt[:, rc_:rc_ + 4], in0=stt[:, rc_:rc_ + 4], in1=EPS_AP[:, 1:5], op=ALU.pow),
                      r=[("st", rc_), ("epsap",)], w=[("st", rc_)])
                P.add("pool", lambda e, otv=otv, rc_=rc_: e.tensor_tensor(out=otv, in0=otv, in1=stt[:, rc_:rc_ + 4].unsqueeze(2).to_broadcast([128, 4, 128]), op=ALU.mult),
                      r=o_.pg() + [("st", rc_)], w=o_.pg())
                P.add("pool", lambda e, s4=s4, mx=mx, o_=o_: e.tensor_tensor(out=mx.ap, in0=o_.ap, in1=gl[s4].ap, op=ALU.mult),
                      r=o_.pg() + gl[s4].pg(), w=mx.pg())

            def PP1(i):
                c = order[i]
                s3 = i % 3
                var = 0 if c == 0 else (2 if c == NCH - 1 else 1)
                zv = zl[s3].ap.rearrange("p (j f) -> p j f", j=3)
                pPv = banks[4][:].rearrange("p (g t) -> p g t", g=4)
                poss = [p_ for p_ in range(3) if 0 <= c + p_ - 1 < NCH]
                for g in range(4):
                    for q_, p_ in enumerate(poss):
                        P.add("pe", lambda e, g=g, p_=p_, q_=q_, zv=zv, pPv=pPv, var=var, poss=poss: e.matmul(
                            pPv[:, g, :], lhsT=zv[:, p_, g * 128:(g + 1) * 128], rhs=BMv[:, (var * 3 + p_) * 4 + g, :],
                            start=(q_ == 0), stop=(q_ == len(poss) - 1)),
                            r=zl[s3].pg() + BM.pg(), w=BK(4))
                pl = pooledT[i % 2]
                P.add("dve", lambda e, pPv=pPv, var=var, pl=pl: e.tensor_tensor(
                    out=pl.ap.rearrange("p (g t) -> p g t", g=4), in0=pPv, in1=RCv[:, var, :, :], op=ALU.mult),
                    r=BK(4) + RCB.pg(), w=pl.pg())

            def PP2(i):
                pl = pooledT[i % 2]
                mt = mixT[i % 3]
                mtv = mt.ap.rearrange("p (k t) -> p k t", k=NK)
                pYv = banks[5][:].rearrange("p (g t) -> p g t", g=4)
                plv = pl.ap.rearrange("p (g t) -> p g t", g=4)
                for g in range(4):
                    P.add("pe", lambda e, g=g, pYv=pYv, plv=plv: e.matmul(pYv[:, g, :], lhsT=PWv[:, g, :], rhs=plv[:, g, :], start=True, stop=True),
                          r=pl.pg() + PW.pg(), w=BK(5))
                P.add("dve", lambda e, pYv=pYv, mtv=mtv: e.tensor_tensor(
                    out=mtv[:, 0:4, :], in0=pYv, in1=psct[:, 0:4].unsqueeze(2).to_broadcast([128, 4, 128]), op=ALU.mult),
                    r=BK(5) + [("psct",)], w=mt.pg(0, 512))

            def S5(i):
                mx = mix[i % 2]
                mt = mixT[i % 3]
                mtv = mt.ap.rearrange("p (k t) -> p k t", k=NK)
                pTv = banks[5][:].bitcast(BF16)[:, 0:512].rearrange("p (n t) -> p n t", n=4)
                for n in range(4):
                    P.add("pe", lambda e, n=n, pTv=pTv, mx=mx: e.transpose(out=pTv[:, n, :], in_=mx.ap[:, n * 128:(n + 1) * 128], identity=ident[:]),
                          r=mx.pg(n * 128, 128) + [("ident",)], w=BK(5))
                P.add("act", lambda e, mtv=mtv, pTv=pTv: e.activation(out=mtv[:, 4:8, :], in_=pTv, func=AF.Copy), r=BK(5), w=mt.pg(512, 512))

            def S6(i):
                c = order[i]
                s4 = i % 4
                mt = mixT[i % 3]
                mtv = mt.ap.rearrange("p (k t) -> p k t", k=NK)
                h = hl[s4]
                for half in range(2):
                    bk = 6 + half
                    for k in range(NK):
                        P.add("pe", lambda e, bk=bk, k=k, half=half, mtv=mtv: e.matmul(
                            banks[bk][:, :], lhsT=mtv[:, k, :], rhs=WOv[:, k, half * 512:(half + 1) * 512], start=(k == 0), stop=(k == NK - 1)),
                            r=mt.pg(k * 128, 128) + WO.pg(k * D + half * 512, 512), w=BK(bk))
                    hh = h.ap[:, half * 512:(half + 1) * 512]
                    P.add("dve", lambda e, bk=bk, hh=hh: e.tensor_tensor(out=hh, in0=banks[bk][:, :], in1=hh, op=ALU.add),
                          r=BK(bk) + h.pg(half * 512, 512), w=h.pg(half * 512, 512))
                P.add("sp", lambda e, h=h, c=c: e.dma_start(out=H_d[c * 128:(c + 1) * 128, :], in_=h.ap),
                      r=h.pg(), w=[("dram", "H", c)], dma=("hl", s4))

            for i in range(min(3, NCH)):
                ld_rec(i)
            if fwd:
                ld_b(0)
            for k in range(-1, NCH + 4):
                if k >= 0 and ok(k + 3):
                    ld_rec(k + 3)
                if fwd and k >= 0 and ok(k + 1):
                    ld_b(k + 1)
                if fwd and ok(k - 2):
                    ld_h(k - 2)
                if ok(k + 1):
                    A1_pe(k + 1)
                if ok(k):
                    A2_s1(k)
                if fwd and ok(k - 1):
                    B1(k - 1)
                if ok(k):
                    A2_upd(k)
                if fwd and ok(k - 2):
                    B2(k - 2)
                if ok(k + 1):
                    A1_dve(k + 1)
                if fwd and ok(k - 4):
                    S6(k - 4)
                if fwd and ok(k - 1):
                    PP1(k - 1)
                if fwd and ok(k - 2):
                    PP2(k - 2)
                if fwd and ok(k - 3):
                    S5(k - 3)
                if ok(k):
                    A2_pso(k)

        if 3 in phases:
            run_dir(1)
        if 4 in phases:
            run_dir(0)

    if 1 in phases:
        load_ffn_w(0, "gd")
        ffn_phase("p1", x_d, "x", H_d, "H", 0, None, 0)
    if 2 in phases:
        phase2()
    if 3 in phases or 4 in phases:
        if 5 in phases:
            load_ffn_w(1, "g")
        phase3()
    if 5 in phases:
        if not (3 in phases or 4 in phases):
            load_ffn_w(1, "g")
        load_ffn_w(1, "d")
        ffn_phase("p4", H_d, "H", out_d, "out", 2, 3, 24)

    P.add("sp", None, r=[k for k in P.lw.keys() if k[0] == "dram"])
    P.emit(nc, es)
    es.close()
    return nc


def _prep_shared(inp):
    ident, maskf, maskb, scanm, Bl, rcl = _consts()
    f = lambda a: np.ascontiguousarray(np.asarray(a, dtype=np.float32))
    hl = f(inp["hgrn_lb"]).reshape(2, 2, 4, 128).transpose(3, 0, 1, 2).reshape(128, 16)
    psc = f(inp["pool_scale"]).reshape(4, 128).T
    nw = np.stack([f(inp["norm_ffn1"])[0], f(inp["norm_mix"])[0], f(inp["norm_ffn2"])[0], f(inp["norm_final"])], 0)
    return {
        "wg1": f(inp["w_ffn1_gate"])[0], "wu1": f(inp["w_ffn1_up"])[0], "wd1": f(inp["w_ffn1_down"])[0],
        "wg2": f(inp["w_ffn2_gate"])[0], "wu2": f(inp["w_ffn2_up"])[0], "wd2": f(inp["w_ffn2_down"])[0],
        "win": f(inp["w_in"])[0], "wout": f(inp["w_out"])[0], "poolw": f(inp["pool_w"])[0],
        "nw": f(nw), "lbp": f(hl), "psc": f(psc), "gn": f(inp["hgrn_gnorm"]).reshape(1, 128),
        "c_ident": ident, "c_maskf": maskf, "c_maskb": maskb, "c_scanm": scanm, "c_bmat": Bl, "c_rc": rcl,
    }


_NC_CACHE = {}


def kernel(**inputs):
    x = np.asarray(inputs["x"], dtype=np.float32)
    B, S, _ = x.shape
    shared = _prep_shared(inputs)
    if S not in _NC_CACHE:
        _NC_CACHE[S] = build(S)
    nc = _NC_CACHE[S]
    in_maps = []
    for b in range(B):
        m = dict(shared)
        m["x"] = np.ascontiguousarray(x[b])
        in_maps.append(m)
    res = run_bass_kernel_spmd(nc, in_maps, core_ids=list(range(B)))
    return np.stack([np.asarray(r["out"], dtype=np.float32) for r in res.results], 0)
```

```python
import numpy as np
from contextlib import ExitStack
import concourse.bass as bass
import concourse.mybir as mybir
from concourse.bass_utils import run_bass_kernel_spmd

F32 = mybir.dt.float32
BF16 = mybir.dt.bfloat16
AF = mybir.ActivationFunctionType
ALU = mybir.AluOpType

D = 1024
DFF = 2816
NF = DFF // 128
NK = D // 128
SEQ = 8192
NCORES = 8
EPS = 1e-6
POOL_WINDOWS = (2, 4, 8, 16)


class _Op:
    __slots__ = ("eng", "fn", "deps", "dma", "needs_inc", "semkey", "val", "waits")


class Prog:
    COMPUTE = ("pe", "act", "dve", "pool")

    def __init__(self):
        self.ops = []
        self.lw = {}
        self.rd = {}

    def add(self, eng, fn, r=(), w=(), dma=None):
        op = _Op()
        op.eng, op.fn, op.dma = eng, fn, dma
        op.needs_inc = False
        op.semkey = None
        op.val = 0
        deps = []
        for k in r:
            x = self.lw.get(k)
            if x is not None:
                deps.append(x)
        for k in w:
            x = self.lw.get(k)
            if x is not None:
                deps.append(x)
            rr = self.rd.get(k)
            if rr:
                for v in rr.values():
                    if isinstance(v, list):
                        deps.extend(v)
                    else:
                        deps.append(v)
        for k in r:
            rr = self.rd.setdefault(k, {})
            if dma is not None:
                rr.setdefault("_dma", []).append(op)
            else:
                rr[eng] = op
        for k in w:
            self.lw[k] = op
            self.rd[k] = {}
        dd = []
        seen = set()
        for d_ in deps:
            if d_ is op or id(d_) in seen:
                continue
            seen.add(id(d_))
            if d_.eng == "pe" and eng == "pe" and d_.dma is None and dma is None:
                continue
            d_.needs_inc = True
            dd.append(d_)
        op.deps = dd
        self.ops.append(op)
        return op

    def finalize(self):
        cnt = {}
        waited = {}
        for op in self.ops:
            key = ("dma", op.dma) if op.dma is not None else ("eng", op.eng)
            need = {}
            for d_ in op.deps:
                if d_.val > need.get(d_.semkey, 0):
                    need[d_.semkey] = d_.val
            ws = []
            wd = waited.setdefault(op.eng, {})
            for sk, v in need.items():
                if wd.get(sk, 0) >= v:
                    continue
                wd[sk] = v
                ws.append((sk, v))
            op.waits = ws
            if op.needs_inc:
                cnt[key] = cnt.get(key, 0) + (16 if op.dma is not None else 1)
                op.semkey = key
                op.val = cnt[key]
        return list(cnt.keys())

    def emit(self, nc, es):
        keys = self.finalize()
        sems = {}
        for i, k in enumerate(keys):
            sems[k] = es.enter_context(nc.semaphore("s%d" % i))
        block = es.enter_context(nc.Block())
        per = {}
        for op in self.ops:
            per.setdefault(op.eng, []).append(op)

        def run(eng_name):
            def body(e):
                for op in per.get(eng_name, []):
                    for sk, v in op.waits:
                        e.wait_ge(sems[sk], v)
                    if op.fn is None:
                        continue
                    ins = op.fn(e)
                    if op.needs_inc:
                        ins.then_inc(sems[op.semkey], 16 if op.dma is not None else 1)
            return body

        block.sync(run("sp"))
        block.scalar(run("act"))
        block.vector(run("dve"))
        block.gpsimd(run("pool"))
        block.tensor(run("pe"))


def _pool_tables():
    Sv = 384
    B = np.zeros((3, 3, 4, 128, 128), np.float32)
    rc = np.zeros((3, 4, 128), np.float32)
    for g, w in enumerate(POOL_WINDOWS):
        full = np.zeros((Sv, Sv), np.float32)
        cnts = np.zeros(Sv, np.float32)
        for t in range(Sv):
            lo = min(max(t - w // 2, 0), Sv)
            hi = min(max(t + w - w // 2, 0), Sv)
            full[lo:hi, t] = 1.0
            cnts[t] = hi - lo
            full[t, t] -= cnts[t]
        for var in range(3):
            c = var
            rc[var, g] = 1.0 / cnts[c * 128:(c + 1) * 128]
            for pos in range(3):
                sc = c + pos - 1
                if 0 <= sc < 3:
                    B[var, pos, g] = full[sc * 128:(sc + 1) * 128, c * 128:(c + 1) * 128]
    return B, rc


def _consts():
    ident = np.eye(128, dtype=np.float32)
    j = np.arange(128)[:, None]
    t = np.arange(128)[None, :]
    maskf = (j <= t).astype(np.float32)
    maskb = (j >= t).astype(np.float32)
    scanm = np.ones((128, 512), np.float32)
    scanm[:, 0::128] = 0.0
    B, rc = _pool_tables()
    Bl = np.ascontiguousarray(B.reshape(36, 128, 128).transpose(1, 0, 2)).reshape(128, 36 * 128)
    rcl = np.ascontiguousarray(rc.reshape(1, 12 * 128))
    return ident, maskf, maskb, scanm, Bl, rcl


PAGE = 256
ARENA_BYTES = 204 * 1024
OFF_WA, OFF_WB, OFF_WC, OFF_WORK = 0, 44 * 1024, 88 * 1024, 132 * 1024


class T_:
    def __init__(self, arena, off, nelem, dt):
        self.off, self.nelem, self.dt = off, nelem, dt
        self.esz = 4 if dt == F32 else 2
        assert off % 4 == 0
        nb = (nelem * self.esz + 3) // 4 * 4
        assert off + nb <= ARENA_BYTES, ("arena overflow", off, nb)
        a = arena[:, off // 4:(off + nb) // 4]
        if dt != F32:
            a = a.bitcast(BF16)
        self.ap = a[:, 0:nelem]

    def pg(self, e0=0, n=None):
        if n is None:
            n = self.nelem - e0
        b0 = self.off + e0 * self.esz
        b1 = self.off + (e0 + n) * self.esz - 1
        return [("pg", i) for i in range(b0 // PAGE, b1 // PAGE + 1)]


class Carver:
    def __init__(self, arena, start, end=ARENA_BYTES):
        self.arena, self.off, self.end = arena, start, end

    def tile(self, nelem, dt):
        t = T_(self.arena, self.off, nelem, dt)
        nb = nelem * t.esz
        self.off += (nb + PAGE - 1) // PAGE * PAGE
        assert self.off <= self.end, ("carver overflow", self.off, self.end)
        return t


def build(S, debug=False, phases=(1, 2, 3, 4, 5), ffn_nt=2):
    assert S % 512 == 0
    NCH = S // 128
    NB2 = S // 512
    nc = bass.Bass("TRN2", target_bir_lowering=False)

    def din(name, shape):
        return nc.dram_tensor(name, shape, F32, kind="ExternalInput").ap()

    skind = "ExternalOutput" if debug else "Internal"

    def dscr(name, shape, dt):
        return nc.dram_tensor(name, shape, dt, kind=skind).ap()

    x_d = din("x", [S, D])
    wg_d = [din("wg1", [D, DFF]), din("wg2", [D, DFF])]
    wu_d = [din("wu1", [D, DFF]), din("wu2", [D, DFF])]
    wd_d = [din("wd1", [DFF, D]), din("wd2", [DFF, D])]
    win_d = din("win", [D, 3072])
    wout_d = din("wout", [D, D])
    poolw_d = din("poolw", [4, 128, 128])
    nw_d = din("nw", [4, D])
    lbp_d = din("lbp", [128, 16])
    psc_d = din("psc", [128, 4])
    gn_d = din("gn", [1, 128])
    ident_d = din("c_ident", [128, 128])
    maskf_d = din("c_maskf", [128, 128])
    maskb_d = din("c_maskb", [128, 128])
    scanm_d = din("c_scanm", [128, 512])
    bmat_d = din("c_bmat", [128, 36 * 128])
    rc_d = din("c_rc", [1, 12 * 128])
    out_d = nc.dram_tensor("out", [S, D], F32, kind="ExternalOutput").ap()

    H_d = dscr("H", [S, D], F32)
    REC_d = [dscr("REC0", [NCH, 128, 1536], BF16), dscr("REC1", [NCH, 128, 1536], BF16)]
    VV_d = dscr("VV", [NCH, 128, 512], BF16)
    G_d = dscr("G", [NCH, 128, 512], F32)
    Z_d = dscr("Z", [NCH, 128, 512], BF16)
    OB_d = dscr("OB", [NCH, 128, 512], F32)

    P = Prog()
    es = ExitStack()

    def sb(name, shape, dt):
        return es.enter_context(nc.sbuf_tensor(name, shape, dt))

    arena = sb("arena", [128, ARENA_BYTES // 4], F32)
    ident = sb("ident", [128, 128], BF16)
    expT = sb("expT", [128, 2 * 4 * NCH], F32)
    expTv = expT[:].rearrange("p (d n c) -> p d n c", d=2, n=4)
    EPS_T = sb("eps_t", [128, 8], F32)
    EPS_AP = EPS_T[:]
    stt = sb("stt", [128, 64], F32)
    lbt = sb("lbt", [128, 16], F32)
    lbs = sb("lbs", [128, 32], F32)
    psct = sb("psct", [128, 4], F32)
    banks = [es.enter_context(nc.psum_tensor("bank%d" % i, [128, 512], F32)) for i in range(8)]

    WA = T_(arena, OFF_WA, NK * DFF, BF16)
    WB = T_(arena, OFF_WB, NK * DFF, BF16)
    WC = T_(arena, OFF_WC, NF * D, BF16)
    WAv = WA.ap.rearrange("p (k f) -> p k f", k=NK)
    WBv = WB.ap.rearrange("p (k f) -> p k f", k=NK)
    WCv = WC.ap.rearrange("p (f d) -> p f d", f=NF)

    P.add("pool", lambda e: e.memset(EPS_AP[:, 0:1], EPS), w=[("epsap",)])
    P.add("pool", lambda e: e.memset(EPS_AP[:, 1:8], -0.5), w=[("epsap",)])
    P.add("pool", lambda e: e.dma_start(out=ident[:], in_=ident_d), w=[("ident",)], dma="ident")

    def BK(i):
        return [("bank", i)]

    def load_ffn_w(li, which):
        if "g" in which:
            for pc in range(4):
                for (T, dstv, src, nm) in ((WA, WAv, wg_d[li], "WA"), (WB, WBv, wu_d[li], "WB")):
                    srcv = src.rearrange("(k p) f -> p k f", p=128)
                    c0, c1 = pc * 704, (pc + 1) * 704
                    wl = []
                    for k in range(NK):
                        wl += T.pg(k * DFF + c0, 704)
                    P.add("pool", lambda e, dstv=dstv, srcv=srcv, c0=c0, c1=c1: e.dma_start(out=dstv[:, :, c0:c1], in_=srcv[:, :, c0:c1]),
                          w=wl, dma=(nm, pc))
        if "d" in which:
            srcv = wd_d[li].rearrange("(f p) d -> p f d", p=128)
            for pc in range(2):
                f0, f1 = pc * 11, (pc + 1) * 11
                P.add("pool", lambda e, srcv=srcv, f0=f0, f1=f1: e.dma_start(out=WCv[:, f0:f1, :], in_=srcv[:, f0:f1, :]),
                      w=WC.pg(f0 * D, 11 * D), dma=("WC", pc))

    def rms_ops(tagp, col, x_ap, x_pg, junk_ap, junk_pg, out_ap, out_pg, nwrow, n=D):
        sq = stt[:, col:col + 1]
        rs = stt[:, 32 + col:33 + col]
        k1, k2 = ("st", col), ("st", 32 + col)
        P.add("act", lambda e: e.activation(out=junk_ap, in_=x_ap, func=AF.Square, accum_out=sq), r=x_pg, w=junk_pg + [k1])
        P.add("pool", lambda e: e.tensor_scalar(out=rs, in0=sq, scalar1=1.0 / n, scalar2=EPS, op0=ALU.mult, op1=ALU.add),
              r=[k1], w=[k2])
        P.add("pool", lambda e: e.tensor_tensor(out=rs, in0=rs, in1=EPS_AP[:, 1:2], op=ALU.pow), r=[k2, ("epsap",)], w=[k2])
        P.add("dve", lambda e: e.scalar_tensor_tensor(out=out_ap, in0=x_ap, scalar=rs, in1=nwrow, op0=ALU.mult, op1=ALU.mult),
              r=x_pg + [k2, ("nwt",)], w=out_pg)

    def ffn_phase(tagp, src_d, src_nm, dst_d, dst_nm, nw_i, final_i, stcol):
        cv = Carver(arena, OFF_WORK)
        nwt = cv.tile(2 * D, F32)
        nwv = nwt.ap.rearrange("p (a d) -> p a d", a=2)
        P.add("sp", lambda e: e.dma_start(out=nwv[:, 0, :], in_=nw_d[nw_i:nw_i + 1, :].rearrange("a d -> (a d)").partition_broadcast(128)),
              w=nwt.pg(0, D) + [("nwt",)], dma=("nwt", tagp, 0))
        if final_i is not None:
            P.add("sp", lambda e: e.dma_start(out=nwv[:, 1, :], in_=nw_d[final_i:final_i + 1, :].rearrange("a d -> (a d)").partition_broadcast(128)),
                  w=nwt.pg(D, D) + [("nwt",)], dma=("nwt", tagp, 1))
        NX = 3 * ffn_nt
        xt = [cv.tile(D, F32) for _ in range(NX)]
        xn = [cv.tile(D, BF16) for _ in range(2)]
        xnT = [cv.tile(NK * 128 * ffn_nt, BF16) for _ in range(2)]
        hT = cv.tile(NF * 128 * ffn_nt, BF16)
        sg = [cv.tile(128 * ffn_nt, BF16) for _ in range(2)]
        TW = 128 * ffn_nt
        blocks = []
        t0 = 0
        ntiles = S // 128
        while t0 < ntiles:
            n = min(ffn_nt, ntiles - t0)
            blocks.append((t0, n))
            t0 += n
        gtile = [0]

        def do_load(b):
            tb, n = blocks[b]
            for i in range(n):
                ti = tb + i
                xa = xt[ti % NX]
                P.add("sp", lambda e, xa=xa, ti=ti: e.dma_start(out=xa.ap, in_=src_d[ti * 128:(ti + 1) * 128, :]),
                      r=[("dram", src_nm, ti)], w=xa.pg(), dma=("xt", tagp, ti % NX))

        def do_norm(b):
            tb, n = blocks[b]
            for i in range(n):
                ti = tb + i
                xa = xt[ti % NX]
                xs = xn[ti % 2]
                rms_ops(tagp, stcol + ti % 4, xa.ap, xa.pg(), xs.ap, xs.pg(), xs.ap, xs.pg(), nwv[:, 0, :])

        def do_transpose(b):
            tb, n = blocks[b]
            xT = xnT[b % 2]
            xTv = xT.ap.rearrange("p (k t) -> p k t", k=NK)
            for i in range(n):
                ti = tb + i
                xs = xn[ti % 2]
                bk = 6 + (ti % 2)
                pv = banks[bk][:].bitcast(BF16).rearrange("p (k t) -> p k t", k=NK)
                for k in range(NK):
                    P.add("pe", lambda e, pv=pv, xs=xs, k=k: e.transpose(out=pv[:, k, :], in_=xs.ap[:, k * 128:(k + 1) * 128], identity=ident[:]),
                          r=xs.pg(k * 128, 128) + [("ident",)], w=BK(bk))
                wl = []
                for k in range(NK):
                    wl += xT.pg(k * TW + i * 128, 128)
                P.add("act", lambda e, pv=pv, xTv=xTv, i=i: e.activation(out=xTv[:, :, i * 128:(i + 1) * 128], in_=pv, func=AF.Copy),
                      r=BK(bk), w=wl)

        def do_gateup(b, mid_hook):
            tb, n = blocks[b]
            ntok = n * 128
            xT = xnT[b % 2]
            xTv = xT.ap.rearrange("p (k t) -> p k t", k=NK)
            hTv = hT.ap.rearrange("p (f t) -> p f t", f=NF)
            for f in range(NF):
                if f == NF // 2 and mid_hook is not None:
                    mid_hook()
                pg_, pu_ = banks[f % 2], banks[2 + f % 2]
                for (W, Wv, pp, bki) in ((WA, WAv, pg_, f % 2), (WB, WBv, pu_, 2 + f % 2)):
                    for k in range(NK):
                        P.add("pe", lambda e, pp=pp, Wv=Wv, k=k, f=f, xTv=xTv, ntok=ntok: e.matmul(
                            pp[:, 0:ntok], lhsT=Wv[:, k, f * 128:(f + 1) * 128], rhs=xTv[:, k, 0:ntok], start=(k == 0), stop=(k == NK - 1)),
                            r=xT.pg(k * TW, ntok) + W.pg(k * DFF + f * 128, 128), w=BK(bki))
                sgt = sg[f % 2]
                P.add("act", lambda e, sgt=sgt, pg_=pg_, ntok=ntok: e.activation(out=sgt.ap[:, 0:ntok], in_=pg_[:, 0:ntok], func=AF.Silu),
                      r=BK(f % 2), w=sgt.pg())
                P.add("dve", lambda e, sgt=sgt, pu_=pu_, f=f, hTv=hTv, ntok=ntok: e.tensor_tensor(
                    out=hTv[:, f, 0:ntok], in0=pu_[:, 0:ntok], in1=sgt.ap[:, 0:ntok], op=ALU.mult),
                    r=sgt.pg() + BK(2 + f % 2), w=hT.pg(f * TW, ntok))

        def do_down(b):
            tb, n = blocks[b]
            hTv = hT.ap.rearrange("p (f t) -> p f t", f=NF)
            for i in range(n):
                ti = tb + i
                xa = xt[ti % NX]
                for half in range(2):
                    j = gtile[0]
                    gtile[0] += 1
                    bk = 4 + j % 2
                    pd_ = banks[bk]
                    for f in range(NF):
                        P.add("pe", lambda e, pd_=pd_, f=f, i=i, half=half: e.matmul(
                            pd_[:, :], lhsT=hTv[:, f, i * 128:(i + 1) * 128], rhs=WCv[:, f, half * 512:(half + 1) * 512],
                            start=(f == 0), stop=(f == NF - 1)),
                            r=hT.pg(f * TW + i * 128, 128) + WC.pg(f * D + half * 512, 512), w=BK(bk))
                    xh = xa.ap[:, half * 512:(half + 1) * 512]
                    P.add("dve", lambda e, pd_=pd_, xh=xh: e.scalar_tensor_tensor(
                        out=xh, in0=pd_[:, :], scalar=0.5, in1=xh, op0=ALU.mult, op1=ALU.add),
                        r=BK(bk) + xa.pg(half * 512, 512), w=xa.pg(half * 512, 512))
                if final_i is not None:
                    xs = xn[ti % 2]
                    rms_ops(tagp, stcol + 4 + ti % 4, xa.ap, xa.pg(), xs.ap, xs.pg(), xa.ap, xa.pg(), nwv[:, 1, :])
                P.add("sp", lambda e, xa=xa, ti=ti: e.dma_start(out=dst_d[ti * 128:(ti + 1) * 128, :], in_=xa.ap),
                      r=xa.pg(), w=[("dram", dst_nm, ti)], dma=("xt", tagp, ti % NX))

        nb = len(blocks)
        do_load(0)
        if nb > 1:
            do_load(1)
        do_norm(0)
        do_transpose(0)
        for b in range(nb):
            if b + 2 < nb:
                do_load(b + 2)
            hook = (lambda b=b: do_norm(b + 1)) if b + 1 < nb else None
            do_gateup(b, hook)
            if b + 1 < nb:
                do_transpose(b + 1)
            do_down(b)

    def phase2():
        cv = Carver(arena, 0)
        WIN = cv.tile(NK * 3072, BF16)
        WINv = WIN.ap.rearrange("p (k c) -> p k c", k=NK)
        srcv = win_d.rearrange("(k p) c -> p k c", p=128)
        for pc in range(6):
            wl = []
            for k in range(NK):
                wl += WIN.pg(k * 3072 + pc * 512, 512)
            P.add("pool", lambda e, pc=pc: e.dma_start(out=WINv[:, :, pc * 512:(pc + 1) * 512], in_=srcv[:, :, pc * 512:(pc + 1) * 512]),
                  w=wl, dma=("WIN", pc))
        nwt = cv.tile(D, F32)
        P.add("sp", lambda e: e.dma_start(out=nwt.ap, in_=nw_d[1:2, :].rearrange("a d -> (a d)").partition_broadcast(128)),
              w=nwt.pg() + [("nwt",)], dma=("nwt", "p2", 0))
        gnb = cv.tile(512, F32)
        gnbv = gnb.ap.rearrange("p (n v) -> p n v", n=4)
        for n in range(4):
            P.add("sp", lambda e, n=n: e.dma_start(out=gnbv[:, n, :], in_=gn_d.rearrange("a d -> (a d)").partition_broadcast(128)),
                  w=gnb.pg(n * 128, 128), dma=("gnb", n))
        scm = cv.tile(512, F32)
        P.add("sp", lambda e: e.dma_start(out=scm.ap, in_=scanm_d), w=scm.pg(), dma=("scm",))
        P.add("sp", lambda e: e.dma_start(out=lbt[:], in_=lbp_d), w=[("lbt",)], dma=("lbt",))
        lbtv = lbt[:].rearrange("p (d l n) -> p d l n", d=2, l=2)
        lbv = lbs[:, 0:8].rearrange("p (d n) -> p d n", d=2)
        P.add("dve", lambda e: e.tensor_tensor(out=lbv, in0=lbtv[:, :, 0, :], in1=lbtv[:, :, 1, :], op=ALU.subtract), r=[("lbt",)], w=[("lbs", 0)])
        P.add("act", lambda e: e.activation(out=lbs[:, 0:8], in_=lbs[:, 0:8], func=AF.Sigmoid), r=[("lbs", 0)], w=[("lbs", 0)])
        P.add("dve", lambda e: e.tensor_scalar(out=lbs[:, 8:16], in0=lbs[:, 0:8], scalar1=-1.0, scalar2=1.0, op0=ALU.mult, op1=ALU.add),
              r=[("lbs", 0)], w=[("lbs", 1)])
        P.add("dve", lambda e: e.tensor_scalar(out=lbs[:, 16:24], in0=lbs[:, 0:8], scalar1=1.0, scalar2=-1.0, op0=ALU.mult, op1=ALU.add),
              r=[("lbs", 0)], w=[("lbs", 2)])
        LBK = [("lbs", 0), ("lbs", 1), ("lbs", 2)]

        ht = [cv.tile(D, F32) for _ in range(4)]
        xn = [cv.tile(D, BF16) for _ in range(2)]
        uT = [cv.tile(NK * 512, BF16) for _ in range(1)]
        sig = [[cv.tile(512, F32) for _ in range(8)] for _ in range(2)]
        tq = [[cv.tile(512, F32) for _ in range(3)] for _ in range(4)]
        qs = [[cv.tile(512, F32) for _ in range(4)] for _ in range(2)]
        RST = [cv.tile(4 * 1536, BF16) for _ in range(2)]
        RSTv = [r.ap.rearrange("p (c k n t) -> p c k n t", c=4, k=3, n=4) for r in RST]
        GST = cv.tile(4 * 512, F32)
        ZST = cv.tile(4 * 512, BF16)
        VST = cv.tile(4 * 512, BF16)
        gtmp = [cv.tile(512, F32) for _ in range(2)]
        junk2 = cv.tile(D, BF16)
        NHT = 4

        def rst_pg(d, c, kk_, n):
            return RST[d].pg(((c * 3 + kk_) * 4 + n) * 128, 128)

        def load_h(b):
            for i in range(4):
                ti = b * 4 + i
                xa = ht[ti % NHT]
                P.add("sp", lambda e, xa=xa, ti=ti: e.dma_start(out=xa.ap, in_=H_d[ti * 128:(ti + 1) * 128, :]),
                      r=[("dram", "H", ti)], w=xa.pg(), dma=("ht", ti % NHT))

        cnt = [0]

        def nb_():
            cnt[0] += 1
            return cnt[0]

        u = uT[0]
        uv = u.ap.rearrange("p (k t) -> p k t", k=NK)

        def norm_stage(b):
            tiles = [b * 4 + i for i in range(4)]
            for ti in tiles:
                xa, xs, col = ht[ti % NHT], xn[ti % 2], 8 + ti % 4
                sq = stt[:, col:col + 1]
                P.add("act", lambda e, xa=xa, sq=sq: e.activation(out=junk2.ap, in_=xa.ap, func=AF.Square, accum_out=sq),
                      r=xa.pg(), w=junk2.pg() + [("st", col)])
            for ti in tiles:
                col = 8 + ti % 4
                sq, rs = stt[:, col:col + 1], stt[:, 32 + col:33 + col]
                P.add("pool", lambda e, sq=sq, rs=rs: e.tensor_scalar(out=rs, in0=sq, scalar1=1.0 / D, scalar2=EPS, op0=ALU.mult, op1=ALU.add),
                      r=[("st", col)], w=[("st", 32 + col)])
                P.add("pool", lambda e, rs=rs: e.tensor_tensor(out=rs, in0=rs, in1=EPS_AP[:, 1:2], op=ALU.pow),
                      r=[("st", 32 + col), ("epsap",)], w=[("st", 32 + col)])

        def xn_op(ti):
            xa, xs, col = ht[ti % NHT], xn[ti % 2], 8 + ti % 4
            rs = stt[:, 32 + col:33 + col]
            P.add("dve", lambda e, xa=xa, xs=xs, rs=rs: e.scalar_tensor_tensor(out=xs.ap, in0=xa.ap, scalar=rs, in1=nwt.ap, op0=ALU.mult, op1=ALU.mult),
                  r=xa.pg() + [("st", 32 + col), ("nwt",)], w=xs.pg())

        def unit_T(b, i):
            ti = b * 4 + i
            xs = xn[ti % 2]
            bk = 6 + (ti % 2)
            pv = banks[bk][:].bitcast(BF16).rearrange("p (k t) -> p k t", k=NK)
            for k in range(NK):
                P.add("pe", lambda e, pv=pv, xs=xs, k=k: e.transpose(out=pv[:, k, :], in_=xs.ap[:, k * 128:(k + 1) * 128], identity=ident[:]),
                      r=xs.pg(k * 128, 128) + [("ident",)], w=BK(bk))
            wl = []
            for k in range(NK):
                wl += u.pg(k * 512 + i * 128, 128)
            P.add("act", lambda e, pv=pv, i=i: e.activation(out=uv[:, :, i * 128:(i + 1) * 128], in_=pv, func=AF.Copy),
                  r=BK(bk), w=wl)
            if i + 2 < 4:
                xn_op(ti + 2)

        def unit_tok(b, c0, kind, i):
            bk = nb_() % 2
            pp = banks[bk]
            for k in range(NK):
                P.add("pe", lambda e, pp=pp, k=k, i=i, c0=c0: e.matmul(
                    pp[:, :], lhsT=uv[:, k, i * 128:(i + 1) * 128], rhs=WINv[:, k, c0:c0 + 512], start=(k == 0), stop=(k == NK - 1)),
                    r=u.pg(k * 512 + i * 128, 128) + WIN.pg(k * 3072 + c0, 512), w=BK(bk))
            if kind == "z":
                P.add("act", lambda e, pp=pp, i=i: e.activation(out=ZST.ap[:, i * 512:(i + 1) * 512], in_=pp[:, :], func=AF.Copy),
                      r=BK(bk), w=ZST.pg(i * 512, 512))
            elif kind == "v":
                P.add("act", lambda e, pp=pp, i=i: e.activation(out=VST.ap[:, i * 512:(i + 1) * 512], in_=pp[:, :], func=AF.Copy),
                      r=BK(bk), w=VST.pg(i * 512, 512))
            else:
                gt = gtmp[i % 2]
                P.add("act", lambda e, pp=pp, gt=gt: e.activation(out=gt.ap, in_=pp[:, :], func=AF.Silu), r=BK(bk), w=gt.pg())
                P.add("pool", lambda e, gt=gt, i=i: e.tensor_tensor(out=GST.ap[:, i * 512:(i + 1) * 512], in0=gt.ap, in1=gnb.ap, op=ALU.mult),
                      r=gt.pg() + gnb.pg(), w=GST.pg(i * 512, 512))

        def stores_A1(b):
            P.add("sp", lambda e, b=b: e.dma_start(out=VV_d[b * 4:(b + 1) * 4].rearrange("c p f -> p c f"), in_=VST.ap.rearrange("p (c f) -> p c f", c=4)),
                  r=VST.pg(), w=[("dram", "VV", b)], dma=("VST",))
            P.add("sp", lambda e, b=b: e.dma_start(out=Z_d[b * 4:(b + 1) * 4].rearrange("c p f -> p c f"), in_=ZST.ap.rearrange("p (c f) -> p c f", c=4)),
                  r=ZST.pg(), w=[("dram", "Z", b)], dma=("ZST",))
            P.add("sp", lambda e, b=b: e.dma_start(out=G_d[b * 4:(b + 1) * 4].rearrange("c p f -> p c f"), in_=GST.ap.rearrange("p (c f) -> p c f", c=4)),
                  r=GST.pg(), w=[("dram", "G", b)], dma=("GST",))

        def unit_q(b, n):
            bk = 2 + nb_() % 3
            pp = banks[bk]
            c0 = 512 + n * 128
            qt = qs[b % 2][n]
            for k in range(NK):
                P.add("pe", lambda e, pp=pp, k=k, c0=c0: e.matmul(
                    pp[:, :], lhsT=WINv[:, k, c0:c0 + 128], rhs=uv[:, k, :], start=(k == 0), stop=(k == NK - 1)),
                    r=u.pg(k * 512, 512) + WIN.pg(k * 3072 + c0, 128), w=BK(bk))
            P.add("act", lambda e, pp=pp, qt=qt: e.activation(out=qt.ap, in_=pp[:, :], func=AF.Silu), r=BK(bk), w=qt.pg())

        def unit_f(b, hd):
            bk = 2 + nb_() % 3
            pp = banks[bk]
            c0 = 1536 + hd * 128
            t0_ = sig[b % 2][hd]
            for k in range(NK):
                P.add("pe", lambda e, pp=pp, k=k, c0=c0: e.matmul(
                    pp[:, :], lhsT=WINv[:, k, c0:c0 + 128], rhs=uv[:, k, :], start=(k == 0), stop=(k == NK - 1)),
                    r=u.pg(k * 512, 512) + WIN.pg(k * 3072 + c0, 128), w=BK(bk))
            P.add("act", lambda e, pp=pp, t0_=t0_: e.activation(out=t0_.ap, in_=pp[:, :], func=AF.Sigmoid), r=BK(bk), w=t0_.pg())

        def chain_steps(b, d):
            steps = []

            def s_fk():
                for n in range(4):
                    hd = d * 4 + n
                    t0_ = sig[b % 2][hd]
                    t1_, t2_, t3_ = tq[n]
                    P.add("dve", lambda e, t0_=t0_, t1_=t1_, hd=hd: e.tensor_scalar(
                        out=t1_.ap, in0=t0_.ap, scalar1=lbs[:, 8 + hd:9 + hd], scalar2=lbs[:, hd:hd + 1], op0=ALU.mult, op1=ALU.add),
                        r=t0_.pg() + LBK, w=t1_.pg())
                    P.add("dve", lambda e, t0_=t0_, t2_=t2_, hd=hd: e.tensor_scalar(
                        out=t2_.ap, in0=t0_.ap, scalar1=lbs[:, 16 + hd:17 + hd], scalar2=lbs[:, 8 + hd:9 + hd], op0=ALU.mult, op1=ALU.add),
                        r=t0_.pg() + LBK, w=t2_.pg())

            def s_ln(n):
                hd = d * 4 + n
                t0_ = sig[b % 2][hd]
                t1_, t2_, t3_ = tq[n]
                P.add("act", lambda e, t1_=t1_: e.activation(out=t1_.ap, in_=t1_.ap, func=AF.Ln), r=t1_.pg(), w=t1_.pg())
                P.add("dve", lambda e, t1_=t1_, t3_=t3_: e.tensor_tensor_scan(
                    out=t3_.ap, data0=scm.ap, data1=t1_.ap, initial=0.0, op0=ALU.mult, op1=ALU.add),
                    r=t1_.pg() + scm.pg(), w=t3_.pg())
                c3 = t3_.ap.rearrange("p (c t) -> p c t", c=4)
                a3 = t0_.ap.rearrange("p (c t) -> p c t", c=4)
                if d == 0:
                    P.add("dve", lambda e, c3=c3, a3=a3: e.tensor_tensor(
                        out=a3, in0=c3, in1=c3[:, :, 127:128].to_broadcast([128, 4, 128]), op=ALU.subtract),
                        r=t3_.pg(), w=t0_.pg())
                else:
                    P.add("dve", lambda e, t0_=t0_, t1_=t1_, t3_=t3_: e.tensor_tensor(out=t0_.ap, in0=t1_.ap, in1=t3_.ap, op=ALU.subtract),
                          r=t1_.pg() + t3_.pg(), w=t0_.pg())

            def s_exp(n):
                hd = d * 4 + n
                t0_ = sig[b % 2][hd]
                t1_, t2_, t3_ = tq[n]
                qt = qs[b % 2][n]
                c3 = t3_.ap.rearrange("p (c t) -> p c t", c=4)
                P.add("act", lambda e, c3=c3, n=n: e.activation(out=expTv[:, d, n, b * 4:(b + 1) * 4], in_=c3[:, :, 127], func=AF.Exp),
                      r=t3_.pg(), w=[("expT", d, n, b)])
                P.add("act", lambda e, t0_=t0_, t1_=t1_: e.activation(out=t1_.ap, in_=t0_.ap, func=AF.Exp), r=t0_.pg(), w=t1_.pg())
                P.add("act", lambda e, t0_=t0_, t3_=t3_: e.activation(out=t3_.ap, in_=t0_.ap, func=AF.Exp, scale=-1.0), r=t0_.pg(), w=t3_.pg())
                wl0, wl1 = [], []
                for c in range(4):
                    wl0 += rst_pg(d, c, 0, n)
                    wl1 += rst_pg(d, c, 1, n)
                P.add("dve", lambda e, n=n, t1_=t1_, qt=qt: e.tensor_tensor(
                    out=RSTv[d][:, :, 0, n, :], in0=qt.ap.rearrange("p (c t) -> p c t", c=4), in1=t1_.ap.rearrange("p (c t) -> p c t", c=4), op=ALU.mult),
                    r=qt.pg() + t1_.pg(), w=wl0)
                P.add("dve", lambda e, n=n, t2_=t2_, t3_=t3_: e.tensor_tensor(
                    out=RSTv[d][:, :, 1, n, :], in0=t2_.ap.rearrange("p (c t) -> p c t", c=4), in1=t3_.ap.rearrange("p (c t) -> p c t", c=4), op=ALU.mult),
                    r=t2_.pg() + t3_.pg(), w=wl1)

            steps.append(s_fk)
            for n in range(4):
                steps.append(lambda n=n: s_ln(n))
            for n in range(4):
                steps.append(lambda n=n: s_exp(n))
            return steps

        def BT(b, d):
            for n in range(4):
                bk = 6 + n % 2
                pv = banks[bk][:].bitcast(BF16)[:, 0:512].rearrange("p (c t) -> p c t", c=4)
                for c in range(4):
                    P.add("pe", lambda e, pv=pv, n=n, c=c: e.transpose(out=pv[:, c, :], in_=RSTv[d][:, c, 1, n, :], identity=ident[:]),
                          r=rst_pg(d, c, 1, n) + [("ident",)], w=BK(bk))
                wl2 = []
                for c in range(4):
                    wl2 += rst_pg(d, c, 2, n)
                if n % 2 == 0:
                    P.add("act", lambda e, pv=pv, n=n: e.activation(out=RSTv[d][:, :, 2, n, :], in_=pv, func=AF.Copy), r=BK(bk), w=wl2)
                else:
                    P.add("dve", lambda e, pv=pv, n=n: e.tensor_copy(out=RSTv[d][:, :, 2, n, :], in_=pv), r=BK(bk), w=wl2)
            P.add("sp", lambda e: e.dma_start(
                out=REC_d[d][b * 4:(b + 1) * 4].rearrange("c p f -> p c f"), in_=RST[d].ap.rearrange("p (c f) -> p c f", c=4)),
                r=RST[d].pg(), w=[("dram", "REC%d" % d, b)], dma=("RST", d))

        def iteration(bA, bB, first):
            units = []
            if bA is not None:
                if first:
                    norm_stage(bA)
                    xn_op(bA * 4)
                    xn_op(bA * 4 + 1)
                for i in range(4):
                    units.append(lambda i=i: unit_T(bA, i))
                for (c0, kind) in ((0, "z"), (1024, "v")):
                    for i in range(4):
                        units.append(lambda c0=c0, kind=kind, i=i: unit_tok(bA, c0, kind, i))
            c0s, c1s = [], []
            if bB is not None:
                c0s = chain_steps(bB, 0)
                c1s = chain_steps(bB, 1)
            ui, ci = 0, 0
            while ui < len(units) or ci < len(c0s):
                if ui < len(units):
                    units[ui]()
                    ui += 1
                    if ui == 4 and bA is not None and bA + 1 < NB2:
                        load_h(bA + 1)
                take = 1
                if ui >= len(units):
                    take = len(c0s)
                for _ in range(take):
                    if ci < len(c0s):
                        c0s[ci]()
                        ci += 1
            if c1s:
                c1s[0]()
            if bA is not None:
                for i in range(4):
                    unit_tok(bA, 2560, "g", i)
                stores_A1(bA)
                if bA + 1 < NB2:
                    norm_stage(bA + 1)
                    xn_op((bA + 1) * 4)
                    xn_op((bA + 1) * 4 + 1)
                for n in range(4):
                    unit_q(bA, n)
            for st_ in c1s[1:5]:
                st_()
            if bA is not None:
                for hd in range(4):
                    unit_f(bA, hd)
            for st_ in c1s[5:9]:
                st_()
            if bA is not None:
                for hd in range(4, 8):
                    unit_f(bA, hd)
            if bB is not None:
                BT(bB, 0)
                BT(bB, 1)

        load_h(0)
        iteration(0, None, True)
        for b in range(NB2):
            iteration(b + 1 if b + 1 < NB2 else None, b, False)


    def phase3():
        cv = Carver(arena, OFF_WC)
        WO = cv.tile(NK * D, BF16)
        WOv = WO.ap.rearrange("p (k d) -> p k d", k=NK)
        P.add("pool", lambda e: e.dma_start(out=WOv, in_=wout_d.rearrange("(k p) d -> p k d", p=128)), w=WO.pg(), dma=("WO",))
        BM = cv.tile(36 * 128, BF16)
        BMv = BM.ap.rearrange("p (m t) -> p m t", m=36)
        P.add("pool", lambda e: e.dma_start(out=BM.ap, in_=bmat_d), w=BM.pg(), dma=("BM",))
        RCB = cv.tile(12 * 128, F32)
        RCv = RCB.ap.rearrange("p (v g t) -> p v g t", v=3, g=4)
        P.add("sp", lambda e: e.dma_start(out=RCB.ap, in_=rc_d.rearrange("a d -> (a d)").partition_broadcast(128)), w=RCB.pg(), dma=("RCB",))
        MK = [cv.tile(128, F32), cv.tile(128, F32)]
        P.add("sp", lambda e: e.dma_start(out=MK[0].ap, in_=maskf_d), w=MK[0].pg(), dma=("MK", 0))
        P.add("sp", lambda e: e.dma_start(out=MK[1].ap, in_=maskb_d), w=MK[1].pg(), dma=("MK", 1))
        PW = cv.tile(512, BF16)
        PWv = PW.ap.rearrange("p (g d) -> p g d", g=4)
        P.add("pool", lambda e: e.dma_start(out=PWv, in_=poolw_d.rearrange("g c d -> c g d")), w=PW.pg(), dma=("PW",))
        P.add("sp", lambda e: e.dma_start(out=psct[:], in_=psc_d), w=[("psct",)], dma=("psct",))
        St = cv.tile(512, F32)
        S1 = cv.tile(512, F32)
        S1b = cv.tile(512, BF16)
        Sv = St.ap.rearrange("p (n v) -> p n v", n=4)
        S1v = S1.ap.rearrange("p (n v) -> p n v", n=4)
        S1bv = S1b.ap.rearrange("p (n v) -> p n v", n=4)
        NR = 4
        rec = [cv.tile(1536, BF16) for _ in range(NR)]
        vv = [cv.tile(512, BF16) for _ in range(NR)]
        PT = [cv.tile(512, BF16) for _ in range(2)]
        obl = [cv.tile(512, F32) for _ in range(3)]
        obst = obl[0:2]
        gl = [cv.tile(512, F32) for _ in range(4)]
        zl = [cv.tile(3 * 512, BF16) for _ in range(3)]
        hl = [cv.tile(D, F32) for _ in range(4)]
        ot = [cv.tile(512, F32) for _ in range(2)]
        junk = cv.tile(512, BF16)
        mix = [cv.tile(512, BF16) for _ in range(2)]
        mixT = [cv.tile(NK * 128, BF16) for _ in range(3)]
        pooledT = [cv.tile(512, BF16) for _ in range(2)]

        def run_dir(d):
            order = list(range(NCH)) if d == 0 else list(range(NCH - 1, -1, -1))
            fwd = (d == 0)
            P.add("pool", lambda e: e.memset(St.ap, 0.0), w=St.pg())
            bA = 0
            bSs = (1, 2)
            bO = 3

            def ok(i):
                return 0 <= i < NCH

            def ld_rec(i):
                c = order[i]
                sl = i % NR
                P.add("sp", lambda e, c=c, sl=sl: e.dma_start(out=rec[sl].ap, in_=REC_d[d][c]),
                      r=[("dram", "REC%d" % d, c // 4)], w=rec[sl].pg(), dma=("rec", sl))
                P.add("sp", lambda e, c=c, sl=sl: e.dma_start(out=vv[sl].ap, in_=VV_d[c]),
                      r=[("dram", "VV", c // 4)], w=vv[sl].pg(), dma=("vv", sl))

            def ld_b(i):
                c = order[i]
                s3 = i % 3
                s4 = i % 4
                P.add("sp", lambda e, c=c, s3=s3: e.dma_start(out=obl[s3].ap, in_=OB_d[c]), r=[("dram", "OB", c)], w=obl[s3].pg(), dma=("obl", s3))
                P.add("sp", lambda e, c=c, s4=s4: e.dma_start(out=gl[s4].ap, in_=G_d[c]), r=[("dram", "G", c // 4)], w=gl[s4].pg(), dma=("gl", s4))
                lo, hi = max(c - 1, 0), min(c + 1, NCH - 1)
                zv = zl[s3].ap.rearrange("p (j f) -> p j f", j=3)
                P.add("sp", lambda e, lo=lo, hi=hi, c=c, zv=zv: e.dma_start(
                    out=zv[:, lo - c + 1:hi - c + 2, :], in_=Z_d[lo:hi + 1].rearrange("c p f -> p c f")),
                    r=[("dram", "Z", lo // 4), ("dram", "Z", hi // 4)], w=zl[s3].pg(), dma=("zl", s3))

            def ld_h(i):
                c = order[i]
                s4 = i % 4
                P.add("sp", lambda e, c=c, s4=s4: e.dma_start(out=hl[s4].ap, in_=H_d[c * 128:(c + 1) * 128, :]),
                      r=[("dram", "H", c)], w=hl[s4].pg(), dma=("hl", s4))

            def views(i):
                sl = i % NR
                rv = rec[sl].ap.rearrange("p (k n t) -> p k n t", k=3, n=4)
                vvv = vv[sl].ap.rearrange("p (n v) -> p n v", n=4)
                return sl, rv, vvv

            def A1_pe(i):
                sl, rv, vvv = views(i)
                pA = banks[bA][:].rearrange("p (n t) -> p n t", n=4)
                bS = bSs[i % 2]
                pS = banks[bS][:].rearrange("p (n t) -> p n t", n=4)
                for n in range(4):
                    P.add("pe", lambda e, pA=pA, rv=rv, n=n: e.matmul(pA[:, n, :], lhsT=rv[:, 1, n, :], rhs=rv[:, 0, n, :], start=True, stop=True),
                          r=rec[sl].pg(), w=BK(bA))
                for n in range(4):
                    P.add("pe", lambda e, pS=pS, rv=rv, vvv=vvv, n=n: e.matmul(pS[:, n, :], lhsT=rv[:, 2, n, :], rhs=vvv[:, n, :], start=True, stop=True),
                          r=rec[sl].pg() + vv[sl].pg(), w=BK(bS))

            def A1_dve(i):
                pA = banks[bA][:].rearrange("p (n t) -> p n t", n=4)
                pt = PT[i % 2]
                ptv = pt.ap.rearrange("p (n t) -> p n t", n=4)
                P.add("dve", lambda e, pA=pA, ptv=ptv: e.tensor_tensor(
                    out=ptv, in0=pA, in1=MK[d].ap.unsqueeze(1).to_broadcast([128, 4, 128]), op=ALU.mult),
                    r=BK(bA) + MK[d].pg(), w=pt.pg())

            def A2_s1(i):
                c = order[i]
                P.add("dve", lambda e, c=c: e.tensor_tensor(out=S1v, in0=Sv, in1=expTv[:, d, :, c:c + 1].to_broadcast([128, 4, 128]), op=ALU.mult),
                      r=St.pg() + [("expT", d, n, c // 4) for n in range(4)], w=S1.pg())
                P.add("act", lambda e: e.activation(out=S1b.ap, in_=S1.ap, func=AF.Copy), r=S1.pg(), w=S1b.pg())

            def A2_upd(i):
                bS = bSs[i % 2]
                pS = banks[bS][:].rearrange("p (n t) -> p n t", n=4)
                P.add("dve", lambda e, pS=pS: e.tensor_tensor(out=Sv, in0=S1v, in1=pS, op=ALU.add), r=S1.pg() + BK(bS), w=St.pg())

            def A2_pso(i):
                c = order[i]
                sl, rv, vvv = views(i)
                pO = banks[bO][:].rearrange("p (n t) -> p n t", n=4)
                pt = PT[i % 2]
                ptv = pt.ap.rearrange("p (n t) -> p n t", n=4)
                for n in range(4):
                    P.add("pe", lambda e, pO=pO, rv=rv, n=n: e.matmul(pO[:, n, :], lhsT=rv[:, 0, n, :], rhs=S1bv[:, n, :], start=True, stop=False),
                          r=rec[sl].pg() + S1b.pg(), w=BK(bO))
                    P.add("pe", lambda e, pO=pO, ptv=ptv, vvv=vvv, n=n: e.matmul(pO[:, n, :], lhsT=ptv[:, n, :], rhs=vvv[:, n, :], start=False, stop=True),
                          r=pt.pg() + vv[sl].pg(), w=BK(bO))
                if not fwd:
                    ob = obst[i % 2]
                    P.add("act", lambda e, ob=ob: e.activation(out=ob.ap, in_=banks[bO][:, :], func=AF.Copy), r=BK(bO), w=ob.pg())
                    P.add("sp", lambda e, ob=ob, c=c: e.dma_start(out=OB_d[c], in_=ob.ap), r=ob.pg(), w=[("dram", "OB", c)], dma=("obl", i % 2))

            def B1(i):
                s3 = i % 3
                o_ = ot[i % 2]
                sc = 16 + 4 * (i % 2)
                rc_ = 48 + 4 * (i % 2)
                P.add("dve", lambda e, s3=s3, o_=o_: e.tensor_tensor(out=o_.ap, in0=banks[bO][:, :], in1=obl[s3].ap, op=ALU.add),
                      r=BK(bO) + obl[s3].pg(), w=o_.pg())
                for n in range(4):
                    P.add("act", lambda e, n=n, o_=o_, sc=sc: e.activation(out=junk.ap[:, n * 128:(n + 1) * 128], in_=o_.ap[:, n * 128:(n + 1) * 128],
                                                                           func=AF.Square, accum_out=stt[:, sc + n:sc + n + 1]),
                          r=o_.pg(n * 128, 128), w=junk.pg(n * 128, 128) + [("st", sc + n)])

            def B2(i):
                o_ = ot[i % 2]
                rc_ = 48 + 4 * (i % 2)
                s4 = i % 4
                otv = o_.ap.rearrange("p (n v) -> p n v", n=4)
                mx = mix[i % 2]
                sc = 16 + 4 * (i % 2)
                P.add("pool", lambda e, sc=sc, rc_=rc_: e.tensor_scalar(out=stt[:, rc_:rc_ + 4], in0=stt[:, sc:sc + 4], scalar1=1.0 / 128, scalar2=EPS,
                                                                       op0=ALU.mult, op1=ALU.add),
                      r=[("st", sc + n) for n in range(4)], w=[("st", rc_)])
                P.add("pool", lambda e, rc_=rc_: e.tensor_tensor(out=stt[:, rc_:rc_ + 4], in0=stt[:, rc_:rc_ + 4], in1=EPS_AP[:, 1:5], op=ALU.pow),
                      r=[("st", rc_), ("epsap",)], w=[("st", rc_)])
                P.add("pool", lambda e, otv=otv, rc_=rc_: e.tensor_tensor(out=otv, in0=otv, in1=stt[:, rc_:rc_ + 4].unsqueeze(2).to_broadcast([128, 4, 128]), op=ALU.mult),
                      r=o_.pg() + [("st", rc_)], w=o_.pg())
                P.add("pool", lambda e, s4=s4, mx=mx, o_=o_: e.tensor_tensor(out=mx.ap, in0=o_.ap, in1=gl[s4].ap, op=ALU.mult),
                      r=o_.pg() + gl[s4].pg(), w=mx.pg())

            def PP1(i):
                c = order[i]
                s3 = i % 3
                var = 0 if c == 0 else (2 if c == NCH - 1 else 1)
                zv = zl[s3].ap.rearrange("p (j f) -> p j f", j=3)
                pPv = banks[4][:].rearrange("p (g t) -> p g t", g=4)
                poss = [p_ for p_ in range(3) if 0 <= c + p_ - 1 < NCH]
                for g in range(4):
                    for q_, p_ in enumerate(poss):
                        P.add("pe", lambda e, g=g, p_=p_, q_=q_, zv=zv, pPv=pPv, var=var, poss=poss: e.matmul(
                            pPv[:, g, :], lhsT=zv[:, p_, g * 128:(g + 1) * 128], rhs=BMv[:, (var * 3 + p_) * 4 + g, :],
                            start=(q_ == 0), stop=(q_ == len(poss) - 1)),
                            r=zl[s3].pg() + BM.pg(), w=BK(4))
                pl = pooledT[i % 2]
                P.add("dve", lambda e, pPv=pPv, var=var, pl=pl: e.tensor_tensor(
                    out=pl.ap.rearrange("p (g t) -> p g t", g=4), in0=pPv, in1=RCv[:, var, :, :], op=ALU.mult),
                    r=BK(4) + RCB.pg(), w=pl.pg())

            def PP2(i):
                pl = pooledT[i % 2]
                mt = mixT[i % 3]
                mtv = mt.ap.rearrange("p (k t) -> p k t", k=NK)
                pYv = banks[5][:].rearrange("p (g t) -> p g t", g=4)
                plv = pl.ap.rearrange("p (g t) -> p g t", g=4)
                for g in range(4):
                    P.add("pe", lambda e, g=g, pYv=pYv, plv=plv: e.matmul(pYv[:, g, :], lhsT=PWv[:, g, :], rhs=plv[:, g, :], start=True, stop=True),
                          r=pl.pg() + PW.pg(), w=BK(5))
                P.add("dve", lambda e, pYv=pYv, mtv=mtv: e.tensor_tensor(
                    out=mtv[:, 0:4, :], in0=pYv, in1=psct[:, 0:4].unsqueeze(2).to_broadcast([128, 4, 128]), op=ALU.mult),
                    r=BK(5) + [("psct",)], w=mt.pg(0, 512))

            def S5(i):
                mx = mix[i % 2]
                mt = mixT[i % 3]
                mtv = mt.ap.rearrange("p (k t) -> p k t", k=NK)
                pTv = banks[5][:].bitcast(BF16)[:, 0:512].rearrange("p (n t) -> p n t", n=4)
                for n in range(4):
                    P.add("pe", lambda e, n=n, pTv=pTv, mx=mx: e.transpose(out=pTv[:, n, :], in_=mx.ap[:, n * 128:(n + 1) * 128], identity=ident[:]),
                          r=mx.pg(n * 128, 128) + [("ident",)], w=BK(5))
                P.add("act", lambda e, mtv=mtv, pTv=pTv: e.activation(out=mtv[:, 4:8, :], in_=pTv, func=AF.Copy), r=BK(5), w=mt.pg(512, 512))

            def S6(i):
                c = order[i]
                s4 = i % 4
                mt = mixT[i % 3]
                mtv = mt.ap.rearrange("p (k t) -> p k t", k=NK)
                h = hl[s4]
                for half in range(2):
                    bk = 6 + half
                    for k in range(NK):
                        P.add("pe", lambda e, bk=bk, k=k, half=half, mtv=mtv: e.matmul(
                            banks[bk][:, :], lhsT=mtv[:, k, :], rhs=WOv[:, k, half * 512:(half + 1) * 512], start=(k == 0), stop=(k == NK - 1)),
                            r=mt.pg(k * 128, 128) + WO.pg(k * D + half * 512, 512), w=BK(bk))
                    hh = h.ap[:, half * 512:(half + 1) * 512]
                    P.add("dve", lambda e, bk=bk, hh=hh: e.tensor_tensor(out=hh, in0=banks[bk][:, :], in1=hh, op=ALU.add),
                          r=BK(bk) + h.pg(half * 512, 512), w=h.pg(half * 512, 512))
                P.add("sp", lambda e, h=h, c=c: e.dma_start(out=H_d[c * 128:(c + 1) * 128, :], in_=h.ap),
                      r=h.pg(), w=[("dram", "H", c)], dma=("hl", s4))

            for i in range(min(3, NCH)):
                ld_rec(i)
            if fwd:
                ld_b(0)
            for k in range(-1, NCH + 4):
                if k >= 0 and ok(k + 3):
                    ld_rec(k + 3)
                if fwd and k >= 0 and ok(k + 1):
                    ld_b(k + 1)
                if fwd and ok(k - 2):
                    ld_h(k - 2)
                if ok(k + 1):
                    A1_pe(k + 1)
                if ok(k):
                    A2_s1(k)
                if fwd and ok(k - 1):
                    B1(k - 1)
                if ok(k):
                    A2_upd(k)
                if fwd and ok(k - 2):
                    B2(k - 2)
                if ok(k + 1):
                    A1_dve(k + 1)
                if fwd and ok(k - 4):
                    S6(k - 4)
                if fwd and ok(k - 1):
                    PP1(k - 1)
                if fwd and ok(k - 2):
                    PP2(k - 2)
                if fwd and ok(k - 3):
                    S5(k - 3)
                if ok(k):
                    A2_pso(k)

        if 3 in phases:
            run_dir(1)
        if 4 in phases:
            run_dir(0)

    if 1 in phases:
        load_ffn_w(0, "gd")
        ffn_phase("p1", x_d, "x", H_d, "H", 0, None, 0)
    if 2 in phases:
        phase2()
    if 3 in phases or 4 in phases:
        if 5 in phases:
            load_ffn_w(1, "g")
        phase3()
    if 5 in phases:
        if not (3 in phases or 4 in phases):
            load_ffn_w(1, "g")
        load_ffn_w(1, "d")
        ffn_phase("p4", H_d, "H", out_d, "out", 2, 3, 24)

    P.add("sp", None, r=[k for k in P.lw.keys() if k[0] == "dram"])
    P.emit(nc, es)
    es.close()
    return nc


def _prep_shared(inp):
    ident, maskf, maskb, scanm, Bl, rcl = _consts()
    f = lambda a: np.ascontiguousarray(np.asarray(a, dtype=np.float32))
    hl = f(inp["hgrn_lb"]).reshape(2, 2, 4, 128).transpose(3, 0, 1, 2).reshape(128, 16)
    psc = f(inp["pool_scale"]).reshape(4, 128).T
    nw = np.stack([f(inp["norm_ffn1"])[0], f(inp["norm_mix"])[0], f(inp["norm_ffn2"])[0], f(inp["norm_final"])], 0)
    return {
        "wg1": f(inp["w_ffn1_gate"])[0], "wu1": f(inp["w_ffn1_up"])[0], "wd1": f(inp["w_ffn1_down"])[0],
        "wg2": f(inp["w_ffn2_gate"])[0], "wu2": f(inp["w_ffn2_up"])[0], "wd2": f(inp["w_ffn2_down"])[0],
        "win": f(inp["w_in"])[0], "wout": f(inp["w_out"])[0], "poolw": f(inp["pool_w"])[0],
        "nw": f(nw), "lbp": f(hl), "psc": f(psc), "gn": f(inp["hgrn_gnorm"]).reshape(1, 128),
        "c_ident": ident, "c_maskf": maskf, "c_maskb": maskb, "c_scanm": scanm, "c_bmat": Bl, "c_rc": rcl,
    }


_NC_CACHE = {}


def kernel(**inputs):
    x = np.asarray(inputs["x"], dtype=np.float32)
    B, S, _ = x.shape
    shared = _prep_shared(inputs)
    if S not in _NC_CACHE:
        _NC_CACHE[S] = build(S)
    nc = _NC_CACHE[S]
    in_maps = []
    for b in range(B):
        m = dict(shared)
        m["x"] = np.ascontiguousarray(x[b])
        in_maps.append(m)
    res = run_bass_kernel_spmd(nc, in_maps, core_ids=list(range(B)))
    return np.stack([np.asarray(r["out"], dtype=np.float32) for r in res.results], 0)
```
